# Optimizing a Trainium2 kernel written in Bass

```python
import jax, jax.numpy as jnp
from jax import lax
import numpy as np

D_MODEL = 1024
BATCH = 8
SEQ = 2048
DEPTH = 4
DEC_BATCH = 128
DEC_SEQ = 1
PAST_LEN = 16384
PAGE_SIZE = 128

META = 16
CHUNK = 64
CONV_W = 4
M_HEADS = 4
M_DK = 128
M_DV = 256
G_HEADS = 8
G_DK = 128
G_DV = 128
M_QK = M_HEADS * M_DK
M_V = M_HEADS * M_DV
G_QK = G_HEADS * G_DK
G_V = G_HEADS * G_DV
D_MIX = M_V + G_V
CONV_CH = 2 * G_QK + G_V
IN_SIZES = (M_QK, M_QK, M_V, M_V, M_V, M_HEADS, M_HEADS, G_QK, G_QK, G_V, G_V, G_HEADS, G_HEADS)
IN_COLS = sum(IN_SIZES)
IN_SPLITS = tuple(int(s) for s in np.cumsum(IN_SIZES)[:-1])
EPS = 1e-6

kernel_name = "hymba_mlstm_gdn_decoder_step"


def _rms_norm(x, w):
    xf = x.astype(jnp.float32)
    y = xf * lax.rsqrt(jnp.mean(xf * xf, axis=-1, keepdims=True) + EPS) * w.astype(jnp.float32)
    return y.astype(x.dtype)


def _head_rms(h):
    return h * lax.rsqrt(jnp.mean(h * h, axis=-1, keepdims=True) + EPS)


def _l2norm(x):
    return x * lax.rsqrt(jnp.sum(x * x, axis=-1, keepdims=True) + EPS)


def _causal_conv(buf, x, w):
    t = x.shape[1]
    xx = jnp.concatenate([buf.astype(x.dtype), x], axis=1)
    y = xx[:, 0:t] * w[0]
    for j in range(1, CONV_W):
        y = y + xx[:, j:j + t] * w[j]
    return y, xx[:, -(CONV_W - 1):]


def _mlstm_chunk(carry, inp):
    C, n, m = carry
    q, k, v, li, lf = inp
    L = q.shape[1]
    li = jnp.swapaxes(li, 1, 2)
    lf = jnp.swapaxes(lf, 1, 2)
    b = jnp.cumsum(lf, axis=-1)
    causal = jnp.tril(jnp.ones((L, L), dtype=bool))
    logw = jnp.where(causal, b[..., :, None] - b[..., None, :] + li[..., None, :], -jnp.inf)
    g = b + m[..., None]
    m_t = jnp.maximum(g, jnp.max(logw, axis=-1))
    w_prev = jnp.exp(g - m_t)
    p = jnp.exp(logw - m_t[..., None]) * jnp.einsum("bthd,bshd->bhts", q, k)
    num = w_prev[..., None] * jnp.einsum("bthd,bhde->bhte", q, C) + jnp.einsum("bhts,bshe->bhte", p, v)
    den = w_prev * jnp.einsum("bthd,bhd->bht", q, n) + jnp.sum(p, axis=-1)
    h = num / jnp.maximum(jnp.abs(den), jnp.exp(-m_t))[..., None]
    a = b[..., -1:] - b + li
    g_end = b[..., -1] + m
    m_new = jnp.maximum(g_end, jnp.max(a, axis=-1))
    ws = jnp.exp(a - m_new[..., None])
    dec = jnp.exp(g_end - m_new)
    C_new = dec[..., None, None] * C + jnp.einsum("bhs,bshd,bshe->bhde", ws, k, v)
    n_new = dec[..., None] * n + jnp.einsum("bhs,bshd->bhd", ws, k)
    return (C_new, n_new, m_new), jnp.swapaxes(h, 1, 2)


def _gdn_chunk(S, inp):
    q, k, v, beta, g = inp
    L = q.shape[1]
    gc = jnp.cumsum(jnp.swapaxes(g, 1, 2), axis=-1)
    bt = jnp.swapaxes(beta, 1, 2)
    incl = jnp.tril(jnp.ones((L, L), dtype=bool))
    strict = jnp.tril(jnp.ones((L, L), dtype=bool), -1)
    gamma = jnp.where(incl, jnp.exp(jnp.where(incl, gc[..., :, None] - gc[..., None, :], 0.0)), 0.0)
    kk = jnp.einsum("bthd,bshd->bhts", k, k)
    tmat = jnp.eye(L, dtype=q.dtype) + jnp.where(strict, bt[..., :, None] * kk * gamma, 0.0)
    rhs = jnp.concatenate([jnp.einsum("bht,bthe->bhte", bt, v),
                           jnp.einsum("bht,bthd->bhtd", bt * jnp.exp(gc), k)], axis=-1)
    sol = lax.linalg.triangular_solve(tmat, rhs, left_side=True, lower=True, unit_diagonal=True)
    dv = v.shape[-1]
    u, w = sol[..., :dv], sol[..., dv:]
    v_new = u - jnp.einsum("bhtd,bhde->bhte", w, S)
    qk = jnp.einsum("bthd,bshd->bhts", q, k) * gamma
    o = jnp.exp(gc)[..., None] * jnp.einsum("bthd,bhde->bhte", q, S) + jnp.einsum("bhts,bhse->bhte", qk, v_new)
    g_end = gc[..., -1]
    S_new = jnp.exp(g_end)[..., None, None] * S + jnp.einsum("bhs,bshd,bhse->bhde",
                                                             jnp.exp(g_end[..., None] - gc), k, v_new)
    return S_new, jnp.swapaxes(o, 1, 2)


def _chunked(step, carry, xs, first_len):
    bsz, t = xs[0].shape[:2]
    carry, y_head = step(carry, tuple(a[:, :first_len] for a in xs))
    rest = t - first_len
    if rest == 0:
        return carry, y_head
    nc = rest // CHUNK
    xs_rest = tuple(jnp.swapaxes(a[:, first_len:].reshape((bsz, nc, CHUNK) + a.shape[2:]), 0, 1) for a in xs)
    carry, ys = lax.scan(step, carry, xs_rest)
    ys = jnp.swapaxes(ys, 0, 1).reshape((bsz, rest) + ys.shape[3:])
    return carry, jnp.concatenate([y_head, ys], axis=1)


def _layer(x, state, first_len, norm_w, w_in, b_ig, b_fg, w_mnorm, conv_w, a_log, dt_bias, w_gnorm, w_out):
    f32 = jnp.float32
    C, n, m, S, conv_buf = state
    bsz, t, _ = x.shape
    u = _rms_norm(x, norm_w) @ w_in
    (q_m, k_m, v_m, o_m, z_m, i_m, f_m, q_g, k_g, v_g, z_g, b_g, a_g) = jnp.split(u, IN_SPLITS, axis=-1)
    qm = q_m.astype(f32).reshape(bsz, t, M_HEADS, M_DK)
    km = k_m.astype(f32).reshape(bsz, t, M_HEADS, M_DK) * (M_DK ** -0.5)
    vm = v_m.astype(f32).reshape(bsz, t, M_HEADS, M_DV)
    li = i_m.astype(f32) + b_ig.astype(f32)
    lf = jax.nn.log_sigmoid(f_m.astype(f32) + b_fg.astype(f32))
    (C, n, m), hm = _chunked(_mlstm_chunk, (C.astype(f32), n.astype(f32), m.astype(f32)),
                             (qm, km, vm, li, lf), first_len)
    ym = (_head_rms(hm).reshape(bsz, t, M_V) * w_mnorm.astype(f32)
          * jax.nn.sigmoid(o_m.astype(f32)) * jax.nn.silu(z_m.astype(f32)))
    qkv, conv_buf = _causal_conv(conv_buf, jnp.concatenate([q_g, k_g, v_g], axis=-1), conv_w)
    qkv = jax.nn.silu(qkv.astype(f32))
    qg, kg, vg = jnp.split(qkv, (G_QK, 2 * G_QK), axis=-1)
    qg = _l2norm(qg.reshape(bsz, t, G_HEADS, G_DK)) * (G_DK ** -0.5)
    kg = _l2norm(kg.reshape(bsz, t, G_HEADS, G_DK))
    vg = vg.reshape(bsz, t, G_HEADS, G_DV)
    beta = jax.nn.sigmoid(b_g.astype(f32))
    g = -jnp.exp(a_log.astype(f32)) * jax.nn.softplus(a_g.astype(f32) + dt_bias.astype(f32))
    S, og = _chunked(_gdn_chunk, S.astype(f32), (qg, kg, vg, beta, g), first_len)
    yg = (_head_rms(og) * w_gnorm.astype(f32)).reshape(bsz, t, G_V) * jax.nn.silu(z_g.astype(f32))
    y = jnp.concatenate([ym, yg], axis=-1).astype(x.dtype) @ w_out
    return x + y, (C, n, m, S, conv_buf)


def setup_inputs(seed: int = 0) -> dict:
    key = jax.random.key(seed)
    ks = jax.random.split(key, 20)
    nrm = jax.random.normal
    dt = jnp.exp(jax.random.uniform(ks[17], (DEPTH, G_HEADS), minval=np.log(1e-3), maxval=np.log(1e-1)))
    return {
        "x_prompt": nrm(ks[0], (BATCH, SEQ, D_MODEL), jnp.float32),
        "x_sample": nrm(ks[1], (DEC_BATCH, DEC_SEQ, D_MODEL), jnp.float32),
        "state_mlstm_C": 0.1 * nrm(ks[2], (DEPTH, DEC_BATCH, M_HEADS, M_DK, M_DV), jnp.float32),
        "state_mlstm_n": 0.5 * nrm(ks[3], (DEPTH, DEC_BATCH, M_HEADS, M_DK), jnp.float32),
        "state_mlstm_m": 0.5 * nrm(ks[4], (DEPTH, DEC_BATCH, M_HEADS), jnp.float32),
        "state_gdn_S": 0.1 * nrm(ks[5], (DEPTH, DEC_BATCH, G_HEADS, G_DK, G_DV), jnp.float32),
        "state_gdn_conv": nrm(ks[6], (DEPTH, DEC_BATCH, CONV_W - 1, CONV_CH), jnp.float32),
        "meta_tokens": nrm(ks[7], (META, D_MODEL), jnp.float32),
        "norm_w": 1.0 + 0.05 * nrm(ks[8], (DEPTH, D_MODEL), jnp.float32),
        "w_in": nrm(ks[9], (DEPTH, D_MODEL, IN_COLS), jnp.float32) * D_MODEL ** -0.5,
        "b_igate": -2.0 + 0.1 * nrm(ks[10], (DEPTH, M_HEADS), jnp.float32),
        "b_fgate": 3.0 + 0.5 * nrm(ks[11], (DEPTH, M_HEADS), jnp.float32),
        "w_mlstm_norm": 1.0 + 0.05 * nrm(ks[12], (DEPTH, M_V), jnp.float32),
        "conv_w": nrm(ks[13], (DEPTH, CONV_W, CONV_CH), jnp.float32) * CONV_W ** -0.5,
        "a_log": jnp.log(jax.random.uniform(ks[14], (DEPTH, G_HEADS), minval=1.0, maxval=16.0)),
        "dt_bias": dt + jnp.log(-jnp.expm1(-dt)),
        "w_gdn_norm": 1.0 + 0.05 * nrm(ks[15], (DEPTH, G_DV), jnp.float32),
        "w_out": nrm(ks[16], (DEPTH, D_MIX, D_MODEL), jnp.float32) * D_MIX ** -0.5,
        "final_norm_w": 1.0 + 0.05 * nrm(ks[18], (D_MODEL,), jnp.float32),
    }


def reference(x_prompt, x_sample, state_mlstm_C, state_mlstm_n, state_mlstm_m, state_gdn_S, state_gdn_conv,
              meta_tokens, norm_w, w_in, b_igate, b_fgate, w_mlstm_norm, conv_w, a_log, dt_bias,
              w_gdn_norm, w_out, final_norm_w):
    f32 = jnp.float32
    dtp = x_prompt.dtype
    bsz = x_prompt.shape[0]
    xp = jnp.concatenate([jnp.broadcast_to(meta_tokens.astype(dtp)[None], (bsz, META, D_MODEL)), x_prompt], axis=1)
    xs = x_sample
    n_new = x_sample.shape[1]
    new_p = []
    new_s = []
    for l in range(DEPTH):
        params = (norm_w[l], w_in[l], b_igate[l], b_fgate[l], w_mlstm_norm[l], conv_w[l],
                  a_log[l], dt_bias[l], w_gdn_norm[l], w_out[l])
        st_p = (jnp.zeros((bsz, M_HEADS, M_DK, M_DV), f32), jnp.zeros((bsz, M_HEADS, M_DK), f32),
                jnp.zeros((bsz, M_HEADS), f32), jnp.zeros((bsz, G_HEADS, G_DK, G_DV), f32),
                jnp.zeros((bsz, CONV_W - 1, CONV_CH), dtp))
        xp, sp = _layer(xp, st_p, META, *params)
        st_s = (state_mlstm_C[l], state_mlstm_n[l], state_mlstm_m[l], state_gdn_S[l], state_gdn_conv[l])
        xs, ss = _layer(xs, st_s, n_new, *params)
        new_p.append(sp)
        new_s.append(ss)
    y_prompt = _rms_norm(xp[:, META:], final_norm_w)
    y_sample = _rms_norm(xs, final_norm_w)
    dts = x_sample.dtype
    p_C = jnp.stack([s[0] for s in new_p]).astype(dtp)
    p_n = jnp.stack([s[1] for s in new_p]).astype(dtp)
    p_m = jnp.stack([s[2] for s in new_p]).astype(dtp)
    p_S = jnp.stack([s[3] for s in new_p]).astype(dtp)
    p_conv = jnp.stack([s[4] for s in new_p]).astype(dtp)
    s_C = jnp.stack([s[0] for s in new_s]).astype(dts)
    s_n = jnp.stack([s[1] for s in new_s]).astype(dts)
    s_m = jnp.stack([s[2] for s in new_s]).astype(dts)
    s_S = jnp.stack([s[3] for s in new_s]).astype(dts)
    s_conv = jnp.stack([s[4] for s in new_s]).astype(dts)
    return (y_prompt, y_sample, p_C, p_n, p_m, p_S, p_conv, s_C, s_n, s_m, s_S, s_conv)
```

```python
from contextlib import ExitStack
import numpy as np
import concourse.bass as bass
import concourse.mybir as mybir
from concourse.bass_utils import run_bass_kernel_spmd

F32 = mybir.dt.float32
BF16 = mybir.dt.bfloat16
ALU = mybir.AluOpType
AF = mybir.ActivationFunctionType
AX = mybir.AxisListType

ENGS = ("pe", "act", "dve", "pool", "sp")
SEM_EPOCH = 30000


class Buf:
    __slots__ = ("name", "t", "lw", "rd", "dsem", "dcnt", "psum")

    def __init__(self, name, t):
        self.name = name
        self.t = t
        self.lw = None
        self.rd = []
        self.dsem = None
        self.dcnt = 0
        self.psum = False

    def __getitem__(self, idx):
        return self.t[idx]


class KB:
    def __init__(self, nc, stack):
        self.nc = nc
        self.stack = stack
        self.ops = {e: [] for e in ENGS}
        self.cnt = {e: 0 for e in ENGS}
        self.esems = {e: [] for e in ENGS}
        self.waited = {e: {} for e in ENGS}
        self.nbuf = 0
        self.maxops = None
        self.rec = None
        self.lead = 70
        self.nops = 0
        self.final = False

    def sb(self, name, shape, dt=F32):
        t = self.stack.enter_context(self.nc.sbuf_tensor(name, list(shape), dt))
        return Buf(name, t)

    def ps(self, name, shape, dt=F32):
        t = self.stack.enter_context(self.nc.psum_tensor(name, list(shape), dt))
        b = Buf(name, t)
        b.psum = True
        return b

    def dram(self, name, shape, dt=F32, kind="ExternalInput"):
        t = self.nc.dram_tensor(name, list(shape), dt, kind=kind)
        return Buf(name, t.ap())

    def _esem(self, eng, idx):
        ep = (idx - 1) // SEM_EPOCH
        lst = self.esems[eng]
        while len(lst) <= ep:
            lst.append(self.stack.enter_context(self.nc.semaphore(f"s_{eng}_{len(lst)}")))
        return lst[ep], idx - ep * SEM_EPOCH

    def _deps(self, reads, writes):
        deps = []
        for b in list(reads) + list(writes):
            if b.lw is not None:
                deps.append(b.lw)
        for b in writes:
            deps.extend(b.rd)
        for b in reads:
            if b.psum:
                deps.extend(b.rd)
        return deps

    def _mark(self, tok, reads, writes):
        for b in reads:
            b.rd.append(tok)
        for b in writes:
            b.lw = tok
            b.rd = []

    def _waits(self, eng, deps):
        wd = self.waited[eng]
        best = {}
        for d in deps:
            if d[0] == "e":
                _, e2, i2 = d
                if e2 == eng and eng in ("pe", "sp"):
                    continue
                key = ("e", e2)
                if i2 > best.get(key, (0, None))[0]:
                    best[key] = (i2, None)
            else:
                _, b, v = d
                v = b.dcnt
                key = ("d", b.name)
                if v > best.get(key, (0, None))[0]:
                    best[key] = (v, b)
        waits = []
        for key, (v, b) in best.items():
            if wd.get(key, 0) >= v:
                continue
            wd[key] = v
            if key[0] == "e":
                waits.append(self._esem(key[1], v))
            else:
                waits.append((b.dsem, v))
        return waits

    def op(self, eng, fn, reads=(), writes=()):
        if self.rec is not None:
            self.rec.append(("op", (eng, fn, tuple(reads), tuple(writes)), {}))
            return
        self.nops += 1
        if self.maxops is not None and self.nops > self.maxops:
            return
        deps = self._deps(reads, writes)
        waits = self._waits(eng, deps)
        self.cnt[eng] += 1
        idx = self.cnt[eng]
        self.ops[eng].append((waits, fn, (self._esem(eng, idx)[0], 1)))
        self._mark(("e", eng, idx), reads, writes)

    def dma(self, out_ap, in_ap, reads, writes, sembuf, eng="sp", **kw):
        if self.rec is not None:
            self.rec.append(("dma", (out_ap, in_ap, tuple(reads), tuple(writes), sembuf), dict(eng=eng, **kw)))
            return
        self.nops += 1
        if self.maxops is not None and self.nops > self.maxops:
            return
        deps = self._deps(reads, writes)
        waits = self._waits(eng, deps)
        if sembuf.dsem is None:
            sembuf.dsem = self.stack.enter_context(self.nc.semaphore(f"d_{sembuf.name}"))
        sembuf.dcnt += 16
        v = sembuf.dcnt
        sem = sembuf.dsem

        def fn(e):
            return e.dma_start(out=out_ap, in_=in_ap, **kw)
        self.ops[eng].append((waits, fn, (sem, 16)))
        self._mark(("d", sembuf, v), reads, writes)

    def play(self, streams):
        idx = [0] * len(streams)
        lead = min(self.lead, len(streams[0])) if len(streams) > 1 else 0
        while idx[0] < lead:
            kind, a, kw = streams[0][idx[0]]
            idx[0] += 1
            if kind == "op":
                self.op(*a)
            else:
                self.dma(*a, **kw)
        live = True
        while live:
            live = False
            for si, st_ in enumerate(streams):
                if idx[si] < len(st_):
                    kind, a, kw = st_[idx[si]]
                    idx[si] += 1
                    live = True
                    if kind == "op":
                        self.op(*a)
                    else:
                        self.dma(*a, **kw)

    def final_wait(self, eng, bufs):
        waits = []
        for b in bufs:
            if b.dsem is not None:
                waits.append((b.dsem, b.dcnt))
        self.ops[eng].append((waits, None, None))

    def emit(self):
        nc = self.nc
        with nc.Block() as block:
            def run(e, lst, is_dma_ok=True):
                for waits, fn, inc in lst:
                    for (s, v) in waits:
                        e.wait_ge(s, v)
                    if fn is None:
                        continue
                    ins = fn(e)
                    if inc is not None:
                        ins.then_inc(inc[0], inc[1])

            @block.tensor
            def _(e):
                run(e, self.ops["pe"])

            @block.scalar
            def _(e):
                run(e, self.ops["act"])

            @block.vector
            def _(e):
                run(e, self.ops["dve"])

            @block.gpsimd
            def _(e):
                run(e, self.ops["pool"])

            @block.sync
            def _(e):
                run(e, self.ops["sp"])


DEPTH = 4
NT = 17
TOK0 = [0] + [16 + 128 * j for j in range(16)]
TLEN = [16] + [128] * 16
STOK = 2064
EPS = 1e-6
NLAYERS_BUILD = DEPTH
INV_DT = F32


def build_program(n_layers=DEPTH, do_samples=True, max_phases=None):
    nc = bass.Bass("TRN2", target_bir_lowering=False)
    st = ExitStack()
    with st:
        k = KB(nc, st)
        import os
        if os.environ.get('KMAXOPS'):
            k.maxops = int(os.environ['KMAXOPS'])
        D = {}
        def din(name, shape):
            D[name] = k.dram(name, shape)
            return D[name]
        def dout(name, shape):
            D[name] = k.dram(name, shape, kind="ExternalOutput")
            return D[name]
        xp = din("xp", [2064, 1024]); xsm = din("xsm", [16, 1024])
        WM = din("WM", [DEPTH, 4, 128, 8, 1026]); WG = din("WG", [DEPTH, 8, 128, 8, 514])
        WO = din("WO", [DEPTH, 16, 128, 1024])
        sC = din("sC", [DEPTH, 16, 4, 128, 256]); sn = din("sn", [DEPTH, 16, 4, 128]); sm = din("sm", [DEPTH, 16, 4])
        sS = din("sS", [DEPTH, 16, 8, 128, 128]); sconv = din("sconv", [DEPTH, 16, 3, 3072])
        cident = din("cident", [128, 128]); cutincl = din("cutincl", [128, 128]); cutstr = din("cutstr", [128, 128])
        cltstr = din("cltstr", [128, 128]); ceye16 = din("ceye16", [128, 16, 16])
        nwT = din("nwT", [128, DEPTH * 8]); wmT = din("wmT", [128, DEPTH * 8]); wgT = din("wgT", [128, DEPTH])
        cwT = din("cwT", [128, DEPTH * 24 * 4]); gpb = din("gpb", [128, DEPTH * 24]); fnw = din("fnw", [128, 1024])
        yp = dout("yp", [2048, 1024]); ys = dout("ys", [16, 1024])
        pC = dout("pC", [DEPTH, 4, 128, 256]); pn = dout("pn", [DEPTH, 4, 128]); pm = dout("pm", [DEPTH, 4])
        pS = dout("pS", [DEPTH, 8, 128, 128]); pconv = dout("pconv", [DEPTH, 3, 3072])
        oC = dout("oC", [DEPTH, 16, 4, 128, 256]); on = dout("on", [DEPTH, 16, 4, 128]); om = dout("om", [DEPTH, 16, 4])
        oS = dout("oS", [DEPTH, 16, 8, 128, 128]); oconv = dout("oconv", [DEPTH, 16, 3, 3072])

        class NS:
            pass
        xt = [k.sb(f"x{t}", [128, 1024]) for t in range(NT + 1)]
        hT_t = st.enter_context(nc.sbuf_tensor("hT", [128, 8, 2080], BF16))
        hT = [Buf(f"hT{t}", hT_t) for t in range(NT + 1)]
        ident = k.sb("ident", [128, 128]); utincl = k.sb("utincl", [128, 128]); utstr = k.sb("utstr", [128, 128])
        ltstr = k.sb("ltstr", [128, 128]); ones = k.sb("ones", [128, 128]); eye16 = k.sb("eye16", [128, 16, 16])
        identb = k.sb("identb", [128, 128], BF16); onesb = k.sb("onesb", [128, 128], BF16)
        nw = k.sb("nw", [128, DEPTH * 8]); wm = k.sb("wm", [128, DEPTH * 8]); wg = k.sb("wg", [128, DEPTH])
        cw = k.sb("cw", [128, DEPTH * 96]); gp = k.sb("gp", [128, DEPTH * 24]); gpd = k.sb("gpd", [128, DEPTH * 24])
        xn = k.sb("xn", [128, 1024], BF16)
        LANE_NAMES = ("Wb Wob g gb mb junk ctmp dg ktok Vp qT qTd kT QKm E G ym yT Cext Cbf Sst Sbf xc cs cacc sq rs vTb vtok "
                      "gl GamT t1 Um Nm Pm Rp QKG vnew kd osb PA PB PC PT").split()
        def mk_lane(i):
            ln = NS()
            sb = lambda name, shape, dt=F32: k.sb(f"{name}_{i}", shape, dt)
            ln.Wb = sb("Wb", [128, 8, 1026], BF16); ln.Wob = sb("Wob", [128, 2, 1024], BF16)
            ln.g = sb("g", [128, 40]); ln.gb = sb("gb", [128, 16]); ln.mb = sb("mb", [128, 1])
            ln.junk = sb("junk", [128, 256], BF16); ln.ctmp = sb("ctmp", [128, 9]); ln.dg = sb("dg", [128, 128])
            ln.ktok = sb("ktok", [128, 128], BF16); ln.Vp = sb("Vp", [128, 258], BF16)
            ln.qT = sb("qT", [128, 128], BF16); ln.qTd = sb("qTd", [128, 128], BF16); ln.kT = sb("kT", [128, 128], BF16)
            ln.QKm = sb("QKm", [128, 128], BF16); ln.E = sb("E", [128, 512]); ln.G = sb("G", [128, 256])
            ln.ym = sb("ym", [128, 256], BF16); ln.yT = sb("yT", [128, 2, 128], BF16)
            ln.Cext = sb("Cext", [128, 257]); ln.Cbf = sb("Cbf", [128, 257], BF16)
            ln.Sst = ln.Cext; ln.Sbf = ln.Cbf
            ln.xc = sb("xc", [128, 3, 131]); ln.cs = sb("cs", [128, 3, 128]); ln.cacc = sb("cacc", [128, 3, 128])
            ln.sq = sb("sq", [128, 256], BF16); ln.rs = sb("rs", [128, 256])
            ln.vTb = sb("vTb", [128, 128], BF16); ln.vtok = ln.Vp
            ln.gl = sb("gl", [128, 128]); ln.GamT = sb("GamT", [128, 128]); ln.t1 = ln.dg
            ln.Um = [sb(f"Um{j}", [128, 128], INV_DT) for j in range(2)]
            ln.Nm = [sb(f"Nm{j}", [128, 128], INV_DT) for j in range(2)]
            ln.Pm = sb("Pm", [128, 128], INV_DT); ln.Rp = sb("Rp", [128, 128], INV_DT)
            ln.QKG = ln.QKm; ln.vnew = sb("vnew", [128, 128], BF16); ln.kd = ln.qTd
            ln.osb = sb("osb", [128, 128])
            ln.PA = k.ps(f"PA_{i}", [128, 512]); ln.PB = k.ps(f"PB_{i}", [128, 512]); ln.PC = k.ps(f"PC_{i}", [128, 512])
            ln.PT = k.ps(f"PT_{i}", [128, 8, 128], BF16)
            return ln
        lanes = [mk_lane(0), mk_lane(1)]
        Cs = k.sb("Cs", [128, 2056])
        CsM = Cs[:, :].rearrange("p (s e) -> p s e", e=257)
        CsG = Cs[:, 0:2048].rearrange("p (s e) -> p s e", e=128)
        CsB = [Buf(f"Cs_s{j}", Cs.t) for j in range(16)]
        msm = k.sb("msm", [16, 4]); hist = k.sb("hist", [16, 3, 128]); histT = k.sb("histT", [128, 9, 16])
        qTf = k.sb("qTf", [128, 16]); kTf = k.sb("kTf", [128, 16])
        QTm = k.sb("QTm", [128, 16, 16]); KTm = k.sb("KTm", [128, 16, 16])
        Kp = k.sb("Kp", [16, 128], BF16); Kmask2 = [k.sb(f"Kmask{j}", [16, 128], BF16) for j in range(2)]; Vx = k.sb("Vx", [16, 258], BF16)
        decb = k.sb("decb", [128, 16]); dg16 = k.sb("dg16", [16, 16])
        pfT = k.sb("pfT", [128, 3, 16])

        def mm(out, lhsT, rhs, R, W, start=True, stop=True):
            k.op("pe", lambda e: e.matmul(out=out, lhsT=lhsT, rhs=rhs, start=start, stop=stop), R, W)
        def tr(out, in_, idn, R, W):
            k.op("pe", lambda e: e.transpose(out=out, in_=in_, identity=idn), R, W)
        def act(out, in_, func, R, W, bias=0.0, scale=1.0, accum=None, eng="act"):
            if accum is None:
                k.op("act", lambda e: e.activation(out=out, in_=in_, func=func, bias=bias, scale=scale), R, W)
            else:
                k.op("act", lambda e: e.activation(out=out, in_=in_, func=func, bias=bias, scale=scale, accum_out=accum), R, W)
        def tt(eng, out, in0, in1, op, R, W):
            k.op(eng, lambda e: e.tensor_tensor(out=out, in0=in0, in1=in1, op=op), R, W)
        def ts(eng, out, in0, s1, op0, R, W, s2=None, op1=None):
            if op1 is None:
                k.op(eng, lambda e: e.tensor_scalar(out=out, in0=in0, scalar1=s1, scalar2=None, op0=op0), R, W)
            else:
                k.op(eng, lambda e: e.tensor_scalar(out=out, in0=in0, scalar1=s1, scalar2=s2, op0=op0, op1=op1), R, W)
        def stt(eng, out, in0, sc, in1, op0, op1, R, W):
            k.op(eng, lambda e: e.scalar_tensor_tensor(out=out, in0=in0, scalar=sc, in1=in1, op0=op0, op1=op1), R, W)
        def cp(eng, out, in_, R, W):
            k.op(eng, lambda e: e.tensor_copy(out=out, in_=in_), R, W)
        def recip(out, in_, R, W):
            k.op("dve", lambda e: e.reciprocal(out=out, in_=in_), R, W)
        def ld(out, in_, dbuf, sbuf, eng="sp", **kw):
            k.dma(out, in_, [dbuf], [sbuf], sbuf, eng=eng, **kw)
        def stv(out, in_, sbuf, dbuf, **kw):
            k.dma(out, in_, [sbuf], [dbuf], sbuf, **kw)
        MUL, ADD, SUB, MAX = ALU.mult, ALU.add, ALU.subtract, ALU.max

        ld(ident[:], cident[:], cident, ident); ld(utincl[:], cutincl[:], cutincl, utincl)
        ld(utstr[:], cutstr[:], cutstr, utstr); ld(ltstr[:], cltstr[:], cltstr, ltstr)
        ld(eye16[:], ceye16[:], ceye16, eye16)
        ld(nw[:], nwT[:], nwT, nw); ld(wm[:], wmT[:], wmT, wm); ld(wg[:], wgT[:], wgT, wg)
        ld(cw[:], cwT[:], cwT, cw); ld(gp[:], gpb[:], gpb, gp)
        k.op("pool", lambda e: e.memset(ones[:], 1.0), [], [ones])
        k.op("pool", lambda e: e.memset(onesb[:], 1.0), [], [onesb])
        cp("dve", identb[:], ident[:], [ident], [identb])
        cp("dve", gpd[:], gp[:], [gp], [gpd])
        for l in range(DEPTH):
            ts("dve", gpd[:, l * 24 + 4:l * 24 + 8], gp[:, l * 24 + 4:l * 24 + 8], -1.0, MUL, [gp], [gpd])
            act(gpd[:, l * 24 + 8:l * 24 + 16], gp[:, l * 24 + 8:l * 24 + 16], AF.Exp, [gp], [gpd])
            ts("dve", gpd[:, l * 24 + 8:l * 24 + 16], gpd[:, l * 24 + 8:l * 24 + 16], -1.0, MUL, [gpd], [gpd])
        for t in range(NT):
            L = TLEN[t]
            ld(xt[t][0:L, :], xp[TOK0[t]:TOK0[t] + L, :], xp, xt[t])
        ld(xt[NT][0:16, :], xsm[:, :], xsm, xt[NT])

        all_tiles = list(range(NT)) + ([NT] if do_samples else [])
        def tinfo(t):
            if t == NT:
                return 16, STOK
            return TLEN[t], TOK0[t]

        def load_weights(ln, l, kind, h):
            xb = ([Cs, msm, hist, pfT, ln.g] + CsB) if do_samples else []
            if kind == "m":
                k.dma(ln.Wb[:, :, 0:1026], WM[l, h], [WM], [ln.Wb] + xb, ln.Wb, eng="pool")
                k.dma(ln.Wob[:, 0:2, :], WO[l, 2 * h:2 * h + 2].rearrange("j p c -> p j c"), [WO], [ln.Wob], ln.Wob, eng="pool")
            else:
                k.dma(ln.Wb[:, :, 0:514], WG[l, h], [WG], [ln.Wb] + xb, ln.Wb, eng="pool")
                k.dma(ln.Wob[:, 0:1, :], WO[l, 8 + h:9 + h].rearrange("j p c -> p j c"), [WO], [ln.Wob], ln.Wob, eng="pool")

        phases = [(l, kind, h) for l in range(n_layers) for (kind, nh) in (("m", 4), ("g", 8)) for h in range(nh)]
        if max_phases is not None:
            phases = phases[:max_phases]

        def prologue(l):
            ln = lanes[0]
            g = ln.g; T0 = ln.PT; junk = xn
            for t in all_tiles:
                L, tok = tinfo(t)
                act(xn[0:L, :], xt[t][0:L, :], AF.Square, [xt[t]], [xn, g], accum=g[0:L, 0:1])
                act(g[0:L, 1:2], g[0:L, 0:1], AF.Ln, [g], [g], bias=EPS, scale=1.0 / 1024.0)
                act(g[0:L, 1:2], g[0:L, 1:2], AF.Exp, [g], [g], scale=-0.5)
                ts("dve", xn[0:L, :], xt[t][0:L, :], g[0:L, 1:2], MUL, [xt[t], g], [xn])
                for kt in range(8):
                    tr(T0[:, kt, 0:L], xn[0:L, kt * 128:(kt + 1) * 128], identb[0:L, 0:L], [xn, identb], [T0])
                tt("dve", hT_t[:, :, tok:tok + L], T0[:, :, 0:L],
                   nw[:, l * 8:(l + 1) * 8].rearrange("p (k o) -> p k o", o=1).to_broadcast([128, 8, L]),
                   MUL, [T0, nw], [hT[t]])

        def run_phase(ln, l, kind, h, tiles):
            (Wb_, Wob_, g, gb, mb, junk, ctmp, dg, ktok, Vp, qT, qTd, kT, QKm, E, G, ym, yT, Cext, Cbf, Sst, Sbf, xc, cs, cacc,
             sq, rs, vTb, vtok, gl, GamT, t1, Um, Nm, Pm, Rp, QKG, vnew, kd, osb, PA, PB, PC, PT) = [getattr(ln, n) for n in LANE_NAMES]
            W = Wb_; Wo = Wob_
            gq = l * 24
            sck = 128.0 ** -0.5

            def rstd_from_ss(L, col_ss, col_out, n):
                act(g[0:L, col_out:col_out + 1], g[0:L, col_ss:col_ss + 1], AF.Ln, [g], [g], bias=EPS, scale=1.0 / n)
                act(g[0:L, col_out:col_out + 1], g[0:L, col_out:col_out + 1], AF.Exp, [g], [g], scale=-0.5)

            def out_proj(t, L, nj, wbuf, wcols):
                for j in range(nj):
                    tr(PT[:, 4 + j, 0:L], ym[0:L, j * 128:(j + 1) * 128], identb[0:L, 0:L], [ym, identb], [PT])
                for j in range(nj):
                    ts("dve", yT[:, j, 0:L], PT[:, 4 + j, 0:L], wcols[j], MUL, [PT, wbuf], [yT])
                for nb, PX in ((0, PA), (1, PB)):
                    for j in range(nj):
                        mm(PX[0:L, 0:512], yT[:, j, 0:L], Wo[:, j, nb * 512:(nb + 1) * 512],
                           [yT, Wo], [PX], start=(j == 0), stop=(j == nj - 1))
                    tt("dve", xt[t][0:L, nb * 512:(nb + 1) * 512], xt[t][0:L, nb * 512:(nb + 1) * 512], PX[0:L, 0:512], ADD, [xt[t], PX], [xt[t]])

            first = (tiles[0] == 0)
            last_prompt = (NT - 1 in tiles)
            if kind == "m":
                if first:
                    k.op("pool", lambda e: e.memset(Cext[:], 0.0), [], [Cext])
                    k.op("pool", lambda e: e.memset(Cbf[:], 0.0), [], [Cbf])
                    k.op("pool", lambda e: e.memset(mb[:], 0.0), [], [mb])
                for t in tiles:
                    L, tok = tinfo(t)
                    smp = (t == NT)
                    if smp:
                        ld(msm[:, :], sm[l, :, :], sm, msm)
                    for kt in range(8):
                        mm(PA[0:L, 0:384], hT_t[:, kt, tok:tok + L], W[:, kt, 128:512], [hT[t], W], [PA], start=(kt == 0), stop=(kt == 7))
                    for kt in range(8):
                        mm(PA[0:L, 384:386], hT_t[:, kt, tok:tok + L], W[:, kt, 1024:1026], [hT[t], W], [PA], start=(kt == 0), stop=(kt == 7))
                    for kt in range(8):
                        mm(PB[0:L, 0:512], hT_t[:, kt, tok:tok + L], W[:, kt, 512:1024], [hT[t], W], [PB], start=(kt == 0), stop=(kt == 7))
                    for j in range(2):
                        for kt in range(8):
                            mm(PC[:, j * L:(j + 1) * L], W[:, kt, j * 128:(j + 1) * 128], hT_t[:, kt, tok:tok + L], [hT[t], W], [PC], start=(kt == 0), stop=(kt == 7))
                    act(E[0:L, :], PB[0:L, 0:512], AF.Exp, [PB], [E], scale=-1.0)
                    ts("dve", E[0:L, :], E[0:L, :], 1.0, ADD, [E], [E])
                    tt("dve", E[0:L, 0:256], E[0:L, 0:256], E[0:L, 256:512], MUL, [E], [E])
                    recip(E[0:L, 0:256], E[0:L, 0:256], [E], [E])
                    tt("dve", G[0:L, :], E[0:L, 0:256], PB[0:L, 256:512], MUL, [E, PB], [G])
                    act(g[0:L, 0:2], PA[0:L, 384:386], AF.Copy, [PA], [g])
                    ts("dve", g[0:L, 2:3], g[0:L, 0:1], gpd[0:L, gq + h:gq + h + 1], ADD, [g, gpd], [g])
                    act(g[0:L, 3:4], g[0:L, 1:2], AF.Exp, [g, gpd], [g], bias=gpd[0:L, gq + 4 + h:gq + 5 + h], scale=-1.0)
                    act(g[0:L, 4:5], g[0:L, 3:4], AF.Ln, [g], [g], bias=1.0)
                    ts("dve", g[0:L, 5:6], g[0:L, 4:5], -1.0, MUL, [g], [g])
                    if not smp:
                        mm(PC[0:L, 384:385], utincl[0:L, 0:L], g[0:L, 5:6], [utincl, g], [PC])
                        mm(PC[:, 385:386], ones[0:L, :], g[0:L, 5:6], [ones, g], [PC])
                        tt("dve", g[0:L, 6:7], g[0:L, 2:3], PC[0:L, 384:385], SUB, [g, PC], [g])
                        ts("dve", dg[0:L, 0:L], ident[0:L, 0:L], g[0:L, 6:7], MUL, [ident, g], [dg])
                        mm(PC[:, 256:256 + L], ones[0:L, :], dg[0:L, 0:L], [ones, dg], [PC])
                        k.op("dve", lambda e, L=L: e.tensor_reduce(out=gb[:, 0:1], in_=PC[:, 256:256 + L], axis=AX.X, op=MAX), [PC], [gb])
                        tt("dve", gb[:, 1:2], gb[:, 0:1], mb[:, 0:1], MAX, [gb, mb], [gb])
                        ts("dve", gb[:, 2:3], gb[:, 1:2], -1.0, MUL, [gb], [gb])
                        act(gb[:, 3:4], mb[:, 0:1], AF.Exp, [mb, gb], [gb], bias=gb[:, 2:3])
                        act(g[0:L, 7:8], g[0:L, 6:7], AF.Exp, [g, gb], [g], bias=gb[0:L, 2:3])
                        tt("dve", gb[:, 4:5], PC[:, 385:386], gb[:, 1:2], ADD, [PC, gb], [gb])
                        act(g[0:L, 8:9], PC[0:L, 384:385], AF.Exp, [PC, gb], [g], bias=gb[0:L, 2:3], scale=-1.0)
                    else:
                        tt("dve", g[0:L, 6:7], g[0:L, 5:6], msm[0:16, h:h + 1], ADD, [g, msm], [g])
                        tt("dve", g[0:L, 20:21], g[0:L, 6:7], g[0:L, 2:3], MAX, [g], [g])
                        ts("dve", g[0:L, 21:22], g[0:L, 20:21], -1.0, MUL, [g], [g])
                        act(g[0:L, 22:23], g[0:L, 6:7], AF.Exp, [g], [g], bias=g[0:L, 21:22])
                        act(g[0:L, 7:8], g[0:L, 2:3], AF.Exp, [g], [g], bias=g[0:L, 21:22])
                        act(g[0:L, 8:9], g[0:L, 20:21], AF.Exp, [g], [g], scale=-1.0)
                        stv(om[l, :, h:h + 1], g[0:16, 20:21], g, om, allow_slow_non_contiguous=True)
                    act(qT[:, 0:L], PC[:, 0:L], AF.Copy, [PC], [qT])
                    ts("dve", kT[:, 0:L], PC[:, L:2 * L], sck, MUL, [PC], [kT])
                    if not smp:
                        ts("dve", ktok[0:L, :], PA[0:L, 0:128], sck, MUL, [PA], [ktok])
                        ts("dve", Vp[0:L, 0:256], PA[0:L, 128:384], g[0:L, 7:8], MUL, [PA, g], [Vp])
                        cp("dve", Vp[0:L, 256:258], g[0:L, 7:8].to_broadcast([L, 2]), [g], [Vp])
                        ts("dve", qTd[:, 0:L], PC[:, 0:L], gb[:, 3:4], MUL, [PC, gb], [qTd])
                        mm(PC[0:L, 256:256 + L], kT[:, 0:L], qT[:, 0:L], [kT, qT], [PC])
                        tt("dve", QKm[0:L, 0:L], PC[0:L, 256:256 + L], utincl[0:L, 0:L], MUL, [PC, utincl], [QKm])
                        mm(PA[0:L, 0:257], qTd[:, 0:L], Cbf[:, :], [qTd, Cbf], [PA], start=True, stop=False)
                        mm(PA[0:L, 0:257], QKm[0:L, 0:L], Vp[0:L, 0:257], [QKm, Vp], [PA], start=False, stop=True)
                    else:
                        ts("dve", Kp[0:16, :], PA[0:16, 0:128], g[0:16, 7:8], MUL, [PA, g], [Kp], s2=sck, op1=MUL)
                        cp("dve", Vx[0:16, 0:256], PA[0:16, 128:384], [PA], [Vx])
                        k.op("pool", lambda e: e.memset(Vx[0:16, 256:258], 1.0), [], [Vx])
                        ts("dve", dg16[:, :], ident[0:16, 0:16], g[0:16, 22:23], MUL, [ident, g], [dg16])
                        mm(PC[:, 256:272], ones[0:16, :], dg16[:, :], [ones, dg16], [PC])
                        cp("dve", decb[:, :], PC[:, 256:272], [PC], [decb])
                        cp("dve", qTf[:, :], PC[:, 0:16], [PC], [qTf])
                        tt("dve", QTm[:, :, :], eye16[:, :, :], qTf[:, 0:16].rearrange("p (o s) -> p o s", o=1).to_broadcast([128, 16, 16]), MUL, [eye16, qTf], [QTm])
                        for hf in range(2):
                            s0 = 8 * hf
                            k.dma(CsM[:, :, 0:256], sC[l, s0:s0 + 8, h].rearrange("s d e -> d s e"), [sC], [Cs] + CsB[0:8], Cs)
                            k.dma(CsM[:, :, 256:257], sn[l, s0:s0 + 8, h, :].rearrange("s (d o) -> d s o", o=1), [sn], [Cs] + CsB[0:8], Cs, allow_slow_non_contiguous=True)
                            for i8 in range(8):
                                i = s0 + i8
                                Kmask = Kmask2[i % 2]
                                PX, pxa = (PB, PB[:, 0:257]) if i % 2 == 0 else (PC, PC[:, 0:257])
                                ts("dve", Kmask[:, :], Kp[:, :], ident[0:16, i:i + 1], MUL, [Kp, ident], [Kmask])
                                mm(pxa, Kmask[:, :], Vx[:, 0:257], [Kmask, Vx], [PX])
                                stt("dve", CsM[:, i8, :], CsM[:, i8, :], decb[:, i:i + 1], pxa, MUL, ADD, [CsB[i8], decb, PX], [CsB[i8]])
                                mm(PA[0:16, 0:257], QTm[:, i, :], CsM[:, i8, :], [QTm, CsB[i8]], [PA], start=(i == 0), stop=(i == 15))
                            k.dma(oC[l, s0:s0 + 8, h].rearrange("s d e -> d s e"), CsM[:, :, 0:256], [Cs] + CsB[0:8], [oC], Cs)
                            k.dma(on[l, s0:s0 + 8, h, :].rearrange("s (d o) -> d s o", o=1), CsM[:, :, 256:257], [Cs] + CsB[0:8], [on], Cs, allow_slow_non_contiguous=True)
                    ts("dve", g[0:L, 9:10], PA[0:L, 256:257], -1.0, MUL, [PA], [g])
                    tt("dve", g[0:L, 9:10], g[0:L, 9:10], PA[0:L, 256:257], MAX, [g, PA], [g])
                    tt("dve", g[0:L, 10:11], g[0:L, 9:10], g[0:L, 8:9], MAX, [g], [g])
                    recip(g[0:L, 11:12], g[0:L, 10:11], [g], [g])
                    act(junk[0:L, 0:256], PA[0:L, 0:256], AF.Square, [PA], [junk, g], accum=g[0:L, 12:13])
                    stt("dve", g[0:L, 13:14], g[0:L, 12:13], g[0:L, 11:12], g[0:L, 11:12], MUL, MUL, [g], [g])
                    rstd_from_ss(L, 13, 15, 256.0)
                    tt("dve", g[0:L, 16:17], g[0:L, 15:16], g[0:L, 11:12], MUL, [g], [g])
                    stt("dve", ym[0:L, 0:256], PA[0:L, 0:256], g[0:L, 16:17], G[0:L, :], MUL, MUL, [PA, g, G], [ym])
                    if not smp:
                        mm(PB[:, 0:257], ktok[0:L, :], Vp[0:L, 0:257], [ktok, Vp], [PB])
                        stt("dve", Cext[:, :], Cext[:, :], gb[:, 3:4], PB[:, 0:257], MUL, ADD, [Cext, gb, PB], [Cext])
                        act(Cbf[:, :], Cext[:, :], AF.Copy, [Cext], [Cbf])
                        cp("dve", mb[:, :], gb[:, 4:5], [gb], [mb])
                    out_proj(t, L, 2, wm, [wm[:, l * 8 + 2 * h + j:l * 8 + 2 * h + j + 1] for j in range(2)])
                if last_prompt:
                    stv(pC[l, h], Cext[:, 0:256], Cext, pC)
                    stv(pn[l, h:h + 1, :].rearrange("o d -> d o"), Cext[:, 256:257], Cext, pn, allow_slow_non_contiguous=True)
                    stv(pm[l:l + 1, h:h + 1], mb[0:1, 0:1], mb, pm)
            else:
                if first:
                    k.op("pool", lambda e: e.memset(Sst[:], 0.0), [], [Sst])
                    k.op("pool", lambda e: e.memset(Sbf[:], 0.0), [], [Sbf])
                    k.op("pool", lambda e: e.memset(xc[:], 0.0), [], [xc])
                cwq = l * 96
                for t in tiles:
                    L, tok = tinfo(t)
                    smp = (t == NT)
                    if smp:
                        k.dma(CsG[:, :, :], sS[l, :, h].rearrange("s d e -> d s e"), [sS], [Cs] + CsB, Cs)
                        if h == 0:
                            k.dma(oconv[l, :, 0:2, :], sconv[l, :, 1:3, :], [sconv], [oconv], oconv)
                    for p in range(3):
                        for kt in range(8):
                            mm(PA[:, p * 128:p * 128 + L], W[:, kt, p * 128:(p + 1) * 128], hT_t[:, kt, tok:tok + L], [hT[t], W], [PA], start=(kt == 0), stop=(kt == 7))
                    for kt in range(8):
                        mm(PB[0:L, 0:130], hT_t[:, kt, tok:tok + L], W[:, kt, 384:514], [hT[t], W], [PB], start=(kt == 0), stop=(kt == 7))
                    if not smp:
                        for p in range(3):
                            act(xc[:, p, 3:3 + L], PA[:, p * 128:p * 128 + L], AF.Copy, [PA], [xc])
                        taps = [[xc[:, p, j:j + L] for j in range(4)] for p in range(3)]
                        creads = [xc]
                    else:
                        for p in range(3):
                            ld(hist[:, :, :], sconv[l, :, :, p * 1024 + h * 128:p * 1024 + (h + 1) * 128], sconv, hist)
                            for j in range(3):
                                tr(PC[:, (p * 3 + j) * 16:(p * 3 + j + 1) * 16], hist[0:16, j, :], ident[0:16, 0:16], [hist, ident], [PC])
                        cp("dve", histT[:, :, :], PC[:, 0:144].rearrange("p (j s) -> p j s", s=16), [PC], [histT])
                        for p in range(3):
                            act(pfT[:, p, :], PA[:, p * 128:p * 128 + 16], AF.Copy, [PA], [pfT])
                        taps = [[histT[:, p * 3 + j, :] for j in range(3)] + [pfT[:, p, :]] for p in range(3)]
                        creads = [histT, pfT]
                        for p in range(3):
                            stv(oconv[l, :, 2, p * 1024 + h * 128:p * 1024 + (h + 1) * 128].rearrange("s c -> c s"), pfT[:, p, :], pfT, oconv, allow_slow_non_contiguous=True)
                    for p in range(3):
                        wc = lambda j, p=p: cw[:, cwq + (p * 8 + h) * 4 + j:cwq + (p * 8 + h) * 4 + j + 1]
                        ts("dve", cacc[:, p, 0:L], taps[p][0], wc(0), MUL, creads + [cw], [cacc])
                        for j in range(1, 4):
                            stt("dve", cacc[:, p, 0:L], taps[p][j], wc(j), cacc[:, p, 0:L], MUL, ADD, creads + [cw, cacc], [cacc])
                    if not smp:
                        if t == NT - 1:
                            for p in range(3):
                                stv(pconv[l, :, p * 1024 + h * 128:p * 1024 + (h + 1) * 128].rearrange("j c -> c j"), xc[:, p, L:L + 3], xc, pconv, allow_slow_non_contiguous=True)
                        cp("dve", ctmp[:, 0:9].rearrange("c (p j) -> c p j", j=3), xc[:, :, L:L + 3], [xc], [ctmp])
                        cp("dve", xc[:, :, 0:3], ctmp[:, 0:9].rearrange("c (p j) -> c p j", j=3), [ctmp], [xc])
                    act(cs[:, :, 0:L], cacc[:, :, 0:L], AF.Exp, [cacc], [cs], scale=-1.0)
                    ts("dve", cs[:, :, 0:L], cs[:, :, 0:L], 1.0, ADD, [cs], [cs])
                    recip(cs[:, :, 0:L], cs[:, :, 0:L], [cs], [cs])
                    tt("dve", cs[:, :, 0:L], cs[:, :, 0:L], cacc[:, :, 0:L], MUL, [cs, cacc], [cs])
                    tt("dve", sq[:, 0:2 * L].rearrange("c (p t) -> c p t", p=2), cs[:, 0:2, 0:L], cs[:, 0:2, 0:L], MUL, [cs], [sq])
                    mm(PB[:, 256:256 + 2 * L], onesb[:, :], sq[:, 0:2 * L], [onesb, sq], [PB])
                    act(rs[:, 0:2 * L], PB[:, 256:256 + 2 * L], AF.Ln, [PB], [rs], bias=EPS)
                    act(rs[:, 0:2 * L], rs[:, 0:2 * L], AF.Exp, [rs], [rs], scale=-0.5)
                    stt("dve", qT[:, 0:L], cs[:, 0, 0:L], 128.0 ** -0.5, rs[:, 0:L], MUL, MUL, [cs, rs], [qT])
                    tt("dve", kT[:, 0:L], cs[:, 1, 0:L], rs[:, L:2 * L], MUL, [cs, rs], [kT])
                    act(vTb[:, 0:L], cs[:, 2, 0:L], AF.Copy, [cs], [vTb])
                    tr(PT[0:L, 0, :], kT[:, 0:L], identb[:, :], [kT, identb], [PT])
                    tr(PT[0:L, 1, :], vTb[:, 0:L], identb[:, :], [vTb, identb], [PT])
                    cp("dve", ktok[0:L, :], PT[0:L, 0, :], [PT], [ktok])
                    cp("dve", vtok[0:L, 0:128], PT[0:L, 1, :], [PT], [vtok])
                    act(g[0:L, 0:1], PB[0:L, 128:129], AF.Exp, [PB], [g], scale=-1.0)
                    ts("dve", g[0:L, 0:1], g[0:L, 0:1], 1.0, ADD, [g], [g])
                    recip(g[0:L, 1:2], g[0:L, 0:1], [g], [g])
                    act(g[0:L, 2:3], PB[0:L, 129:130], AF.Exp, [PB, gpd], [g], bias=gpd[0:L, gq + 16 + h:gq + 17 + h])
                    act(g[0:L, 3:4], g[0:L, 2:3], AF.Ln, [g], [g], bias=1.0)
                    ts("dve", g[0:L, 5:6], g[0:L, 3:4], gpd[0:L, gq + 8 + h:gq + 9 + h], MUL, [g, gpd], [g])
                    act(E[0:L, 0:128], PB[0:L, 0:128], AF.Exp, [PB], [E], scale=-1.0)
                    ts("dve", E[0:L, 0:128], E[0:L, 0:128], 1.0, ADD, [E], [E])
                    recip(E[0:L, 0:128], E[0:L, 0:128], [E], [E])
                    tt("dve", G[0:L, 0:128], E[0:L, 0:128], PB[0:L, 0:128], MUL, [E, PB], [G])
                    if not smp:
                        mm(PB[0:L, 130:131], utincl[0:L, 0:L], g[0:L, 5:6], [utincl, g], [PB])
                        mm(PB[:, 131:132], ones[0:L, :], g[0:L, 5:6], [ones, g], [PB])
                        act(g[0:L, 6:7], PB[0:L, 130:131], AF.Exp, [PB], [g])
                        ts("dve", g[0:L, 7:8], g[0:L, 6:7], -1.0, MUL, [g], [g])
                        act(gb[:, 5:6], PB[:, 131:132], AF.Exp, [PB], [gb])
                        cp("dve", gb[:, 6:7], PB[:, 131:132], [PB], [gb])
                        act(g[0:L, 8:9], PB[0:L, 130:131], AF.Exp, [PB, gb], [g], bias=gb[0:L, 6:7], scale=-1.0)
                        ts("dve", gl[0:L, 0:L], ltstr[0:L, 0:L], g[0:L, 5:6], MUL, [ltstr, g], [gl])
                        mm(PC[0:L, 0:L], gl[0:L, 0:L], utincl[0:L, 0:L], [gl, utincl], [PC])
                        act(GamT[0:L, 0:L], PC[0:L, 0:L], AF.Exp, [PC], [GamT])
                        mm(PB[0:L, 256:256 + L], kT[:, 0:L], kT[:, 0:L], [kT], [PB])
                        mm(PB[0:L, 384:384 + L], kT[:, 0:L], qT[:, 0:L], [kT, qT], [PB])
                        tt("dve", t1[0:L, 0:L], PB[0:L, 256:256 + L], GamT[0:L, 0:L], MUL, [PB, GamT], [t1])
                        stt("dve", Um[0][0:L, 0:L], t1[0:L, 0:L], g[0:L, 1:2], utstr[0:L, 0:L], MUL, MUL, [t1, g, utstr], [Um[0]])
                        tt("dve", t1[0:L, 0:L], PB[0:L, 384:384 + L], GamT[0:L, 0:L], MUL, [PB, GamT], [t1])
                        tt("dve", QKG[0:L, 0:L], t1[0:L, 0:L], utincl[0:L, 0:L], MUL, [t1, utincl], [QKG])
                        idn_i = ident if INV_DT == F32 else identb
                        if INV_DT == F32:
                            tr(PC[0:L, 128:128 + L], Um[0][0:L, 0:L], idn_i[0:L, 0:L], [Um[0], idn_i], [PC])
                            cp("dve", Nm[0][0:L, 0:L], PC[0:L, 128:128 + L], [PC], [Nm[0]])
                        else:
                            tr(PT[0:L, 2, 0:L], Um[0][0:L, 0:L], idn_i[0:L, 0:L], [Um[0], idn_i], [PT])
                            cp("dve", Nm[0][0:L, 0:L], PT[0:L, 2, 0:L], [PT], [Nm[0]])
                        stt("dve", Pm[0:L, 0:L], Um[0][0:L, 0:L], -1.0, ident[0:L, 0:L], MUL, ADD, [Um[0], ident], [Pm])
                        nlev = {128: 6, 16: 3}[L]
                        cur = 0
                        for lev in range(nlev):
                            nx = 1 - cur
                            mm(PC[0:L, 128:128 + L], Um[cur][0:L, 0:L], Nm[cur][0:L, 0:L], [Um[cur], Nm[cur]], [PC])
                            act(Nm[nx][0:L, 0:L], PC[0:L, 128:128 + L], AF.Copy, [PC], [Nm[nx]])
                            if lev < nlev - 1:
                                mm(PC[0:L, 256:256 + L], Nm[cur][0:L, 0:L], Um[cur][0:L, 0:L], [Um[cur], Nm[cur]], [PC])
                                act(Um[nx][0:L, 0:L], PC[0:L, 256:256 + L], AF.Copy, [PC], [Um[nx]])
                            mm(PC[0:L, 384:384 + L], Nm[nx][0:L, 0:L], Pm[0:L, 0:L], [Nm[nx], Pm], [PC])
                            tt("dve", Pm[0:L, 0:L], Pm[0:L, 0:L], PC[0:L, 384:384 + L], ADD, [Pm, PC], [Pm])
                            cur = nx
                        mm(PA[0:L, 0:128], kT[:, 0:L], Sbf[:, 0:128], [kT, Sbf], [PA])
                        stt("dve", Rp[0:L, :], PA[0:L, 0:128], g[0:L, 7:8], vtok[0:L, 0:128], MUL, ADD, [PA, g, vtok], [Rp])
                        mm(PA[0:L, 128:256], Pm[0:L, 0:L], Rp[0:L, :], [Pm, Rp], [PA])
                        ts("dve", vnew[0:L, :], PA[0:L, 128:256], g[0:L, 1:2], MUL, [PA, g], [vnew])
                        mm(PA[0:L, 256:384], qT[:, 0:L], Sbf[:, 0:128], [qT, Sbf], [PA])
                        ts("dve", osb[0:L, :], PA[0:L, 256:384], g[0:L, 6:7], MUL, [PA, g], [osb])
                        mm(PA[0:L, 384:512], QKG[0:L, 0:L], vnew[0:L, :], [QKG, vnew], [PA])
                        tt("dve", osb[0:L, :], osb[0:L, :], PA[0:L, 384:512], ADD, [osb, PA], [osb])
                        ts("dve", kd[0:L, :], ktok[0:L, :], g[0:L, 8:9], MUL, [ktok, g], [kd])
                        mm(PC[:, 0:128], kd[0:L, :], vnew[0:L, :], [kd, vnew], [PC])
                        stt("dve", Sst[:, 0:128], Sst[:, 0:128], gb[:, 5:6], PC[:, 0:128], MUL, ADD, [Sst, gb, PC], [Sst])
                        act(Sbf[:, 0:128], Sst[:, 0:128], AF.Copy, [Sst], [Sbf])
                    else:
                        act(g[0:16, 6:7], g[0:16, 5:6], AF.Exp, [g], [g])
                        ts("dve", g[0:16, 7:8], g[0:16, 6:7], -1.0, MUL, [g], [g])
                        ts("dve", dg16[:, :], ident[0:16, 0:16], g[0:16, 6:7], MUL, [ident, g], [dg16])
                        mm(PC[:, 384:400], ones[0:16, :], dg16[:, :], [ones, dg16], [PC])
                        cp("dve", decb[:, :], PC[:, 384:400], [PC], [decb])
                        cp("dve", qTf[:, :], qT[:, 0:16], [qT], [qTf])
                        cp("dve", kTf[:, :], kT[:, 0:16], [kT], [kTf])
                        tt("dve", QTm[:, :, :], eye16[:, :, :], qTf[:, 0:16].rearrange("p (o s) -> p o s", o=1).to_broadcast([128, 16, 16]), MUL, [eye16, qTf], [QTm])
                        tt("dve", KTm[:, :, :], eye16[:, :, :], kTf[:, 0:16].rearrange("p (o s) -> p o s", o=1).to_broadcast([128, 16, 16]), MUL, [eye16, kTf], [KTm])
                        for i in range(16):
                            mm(PA[0:16, 0:128], KTm[:, i, :], CsG[:, i, :], [KTm, CsB[i]], [PA], start=(i == 0), stop=(i == 15))
                        stt("dve", Rp[0:16, :], PA[0:16, 0:128], g[0:16, 7:8], vtok[0:16, 0:128], MUL, ADD, [PA, g, vtok], [Rp])
                        ts("dve", vnew[0:16, :], Rp[0:16, :], g[0:16, 1:2], MUL, [Rp, g], [vnew])
                        for i in range(16):
                            Kmask = Kmask2[i % 2]
                            PX, pxa = (PC, PC[:, 256:384]) if i % 2 == 0 else (PB, PB[:, 256:384])
                            ts("dve", Kmask[:, :], ktok[0:16, :], ident[0:16, i:i + 1], MUL, [ktok, ident], [Kmask])
                            mm(pxa, Kmask[:, :], vnew[0:16, :], [Kmask, vnew], [PX])
                            stt("dve", CsG[:, i, :], CsG[:, i, :], decb[:, i:i + 1], pxa, MUL, ADD, [CsB[i], decb, PX], [CsB[i]])
                            mm(PA[0:16, 256:384], QTm[:, i, :], CsG[:, i, :], [QTm, CsB[i]], [PA], start=(i == 0), stop=(i == 15))
                        k.dma(oS[l, :, h].rearrange("s d e -> d s e"), CsG[:, :, :], [Cs] + CsB, [oS], Cs)
                        cp("dve", osb[0:16, :], PA[0:16, 256:384], [PA], [osb])
                    act(junk[0:L, 0:128], osb[0:L, :], AF.Square, [osb], [junk, g], accum=g[0:L, 12:13])
                    rstd_from_ss(L, 12, 15, 128.0)
                    stt("dve", ym[0:L, 0:128], osb[0:L, :], g[0:L, 15:16], G[0:L, 0:128], MUL, MUL, [osb, g, G], [ym])
                    out_proj(t, L, 1, wg, [wg[:, l:l + 1]])
                if last_prompt:
                    stv(pS[l, h], Sst[:, 0:128], Sst, pS)

        prompt_tiles = list(range(NT))
        pi = 0
        while pi < len(phases):
            pair = phases[pi:pi + 2]
            if pair[0][1] == "m" and pair[0][2] == 0:
                prologue(pair[0][0])
            streams = []
            for li, (l, kind, h) in enumerate(pair):
                load_weights(lanes[li], l, kind, h)
            for li, (l, kind, h) in enumerate(pair):
                k.rec = []
                run_phase(lanes[li], l, kind, h, prompt_tiles)
                streams.append(k.rec)
                k.rec = None
            k.play(streams)
            if do_samples:
                for li, (l, kind, h) in enumerate(pair):
                    run_phase(lanes[li], l, kind, h, [NT])
            pi += 2
        g = lanes[0].g; junk = xn; mb = lanes[0].mb
        k.dma(CsM[:, 0:4, 0:256], fnw[:, :].rearrange("p (a b) -> p a b", a=4), [fnw], [Cs] + CsB, Cs)
        for t in all_tiles:
            L, tok = tinfo(t)
            act(junk[0:L, :], xt[t][0:L, :], AF.Square, [xt[t]], [junk, g], accum=g[0:L, 0:1])
            act(g[0:L, 1:2], g[0:L, 0:1], AF.Ln, [g], [g], bias=EPS, scale=1.0 / 1024.0)
            act(g[0:L, 1:2], g[0:L, 1:2], AF.Exp, [g], [g], scale=-0.5)
            for a4 in range(4):
                stt("dve", xt[t][0:L, a4 * 256:(a4 + 1) * 256], xt[t][0:L, a4 * 256:(a4 + 1) * 256], g[0:L, 1:2], CsM[0:L, a4, 0:256], MUL, MUL, [xt[t], g, Cs], [xt[t]])
            if t == NT:
                stv(ys[:, :], xt[t][0:16, :], xt[t], ys)
            elif t > 0:
                stv(yp[tok - 16:tok - 16 + L, :], xt[t][0:L, :], xt[t], yp)
        allb = xt + [Cs, pfT, oconv] + [b for ln in lanes for b in (ln.Cext, ln.mb, ln.Sst, ln.xc, ln.g)]
        k.final_wait("sp", allb)
        print('total ops', k.nops)
        k.emit()
    return nc


def _consts():
    i = np.arange(128)
    c = {}
    c["cident"] = np.eye(128, dtype=np.float32)
    c["cutincl"] = (i[:, None] <= i[None, :]).astype(np.float32)
    c["cutstr"] = (i[:, None] < i[None, :]).astype(np.float32)
    c["cltstr"] = (i[:, None] > i[None, :]).astype(np.float32)
    c["ceye16"] = np.ascontiguousarray(np.broadcast_to(np.eye(16, dtype=np.float32)[None], (128, 16, 16)))
    return c


def _pk(w):
    C = w.shape[1]
    return w.reshape(8, 128, C).transpose(1, 0, 2)


_NC_CACHE = {}


def kernel(x_prompt, x_sample, state_mlstm_C, state_mlstm_n, state_mlstm_m, state_gdn_S, state_gdn_conv,
           meta_tokens, norm_w, w_in, b_igate, b_fgate, w_mlstm_norm, conv_w, a_log, dt_bias,
           w_gdn_norm, w_out, final_norm_w, _n_layers=DEPTH, _do_samples=True, _max_phases=None):
    f = lambda a: np.ascontiguousarray(np.asarray(a, dtype=np.float32))
    x_prompt, x_sample, meta_tokens, w_in, w_out = f(x_prompt), f(x_sample), f(meta_tokens), f(w_in), f(w_out)
    OFF = dict(qm=0, km=512, vm=1024, om=2048, zm=3072, im=4096, fm=4100, qg=4104, kg=5128, vg=6152, zg=7176, bg=8200, ag=8208)
    WM = np.empty((DEPTH, 4, 128, 8, 1026), np.float32)
    WG = np.empty((DEPTH, 8, 128, 8, 514), np.float32)
    for l in range(DEPTH):
        w = w_in[l]
        for h in range(4):
            cols = np.concatenate([np.arange(OFF["qm"] + h * 128, OFF["qm"] + (h + 1) * 128),
                                   np.arange(OFF["km"] + h * 128, OFF["km"] + (h + 1) * 128),
                                   np.arange(OFF["vm"] + h * 256, OFF["vm"] + (h + 1) * 256),
                                   np.arange(OFF["om"] + h * 256, OFF["om"] + (h + 1) * 256),
                                   np.arange(OFF["zm"] + h * 256, OFF["zm"] + (h + 1) * 256),
                                   [OFF["im"] + h, OFF["fm"] + h]]).astype(np.int64)
            WM[l, h] = _pk(w[:, cols])
        for h in range(8):
            cols = np.concatenate([np.arange(OFF[kk] + h * 128, OFF[kk] + (h + 1) * 128) for kk in ("qg", "kg", "vg", "zg")]
                                  + [[OFF["bg"] + h, OFF["ag"] + h]]).astype(np.int64)
            WG[l, h] = _pk(w[:, cols])
    WO = np.ascontiguousarray(w_out.reshape(DEPTH, 16, 128, 1024))
    nwT = np.ascontiguousarray(f(norm_w).reshape(DEPTH, 8, 128).transpose(2, 0, 1).reshape(128, DEPTH * 8))
    wmT = np.ascontiguousarray(f(w_mlstm_norm).reshape(DEPTH, 8, 128).transpose(2, 0, 1).reshape(128, DEPTH * 8))
    wgT = np.ascontiguousarray(f(w_gdn_norm).T)
    cwT = np.ascontiguousarray(f(conv_w).reshape(DEPTH, 4, 3, 8, 128).transpose(4, 0, 2, 3, 1).reshape(128, DEPTH * 96))
    gpr = np.concatenate([f(b_igate), f(b_fgate), f(a_log), f(dt_bias)], axis=1).reshape(1, DEPTH * 24)
    gpb = np.ascontiguousarray(np.broadcast_to(gpr, (128, DEPTH * 24)))
    fnw = np.ascontiguousarray(np.broadcast_to(f(final_norm_w)[None], (128, 1024)))
    consts = _consts()
    key = (_n_layers, _do_samples, _max_phases)
    if key not in _NC_CACHE:
        _NC_CACHE[key] = build_program(_n_layers, _do_samples, _max_phases)
    nc = _NC_CACHE[key]
    sC_, sn_, sm_, sS_, sv_ = f(state_mlstm_C), f(state_mlstm_n), f(state_mlstm_m), f(state_gdn_S), f(state_gdn_conv)
    in_maps = []
    for c in range(8):
        sl = slice(16 * c, 16 * c + 16)
        m = dict(xp=np.concatenate([meta_tokens, x_prompt[c]], axis=0), xsm=np.ascontiguousarray(x_sample[sl, 0, :]),
                 WM=WM, WG=WG, WO=WO,
                 sC=np.ascontiguousarray(sC_[:, sl]), sn=np.ascontiguousarray(sn_[:, sl]), sm=np.ascontiguousarray(sm_[:, sl]),
                 sS=np.ascontiguousarray(sS_[:, sl]), sconv=np.ascontiguousarray(sv_[:, sl]),
                 nwT=nwT, wmT=wmT, wgT=wgT, cwT=cwT, gpb=gpb, fnw=fnw)
        m.update(consts)
        in_maps.append(m)
    res = run_bass_kernel_spmd(nc, in_maps, core_ids=list(range(8)))
    R = res.results
    cat = lambda name, ax: np.concatenate([np.expand_dims(r[name], ax) if False else r[name] for r in R], axis=ax)
    y_prompt = np.stack([r["yp"] for r in R], 0)
    y_sample = np.concatenate([r["ys"] for r in R], 0)[:, None, :]
    p_C = np.stack([r["pC"] for r in R], 1); p_n = np.stack([r["pn"] for r in R], 1); p_m = np.stack([r["pm"] for r in R], 1)
    p_S = np.stack([r["pS"] for r in R], 1); p_conv = np.stack([r["pconv"] for r in R], 1)
    s_C = np.concatenate([r["oC"] for r in R], 1); s_n = np.concatenate([r["on"] for r in R], 1)
    s_m = np.concatenate([r["om"] for r in R], 1); s_S = np.concatenate([r["oS"] for r in R], 1)
    s_conv = np.concatenate([r["oconv"] for r in R], 1)
    outs = (y_prompt, y_sample, p_C, p_n, p_m, p_S, p_conv, s_C, s_n, s_m, s_S, s_conv)
    return tuple(np.ascontiguousarray(o.astype(np.float32)) for o in outs)
```

```python
from contextlib import ExitStack
import numpy as np
import concourse.bass as bass
import concourse.mybir as mybir
from concourse.bass_utils import run_bass_kernel_spmd

F32 = mybir.dt.float32
BF16 = mybir.dt.bfloat16
ALU = mybir.AluOpType
AF = mybir.ActivationFunctionType
AX = mybir.AxisListType

ENGS = ("pe", "act", "dve", "pool", "sp")
SEM_EPOCH = 30000


class Buf:
    __slots__ = ("name", "t", "lw", "rd", "dsem", "dcnt", "psum")

    def __init__(self, name, t):
        self.name = name
        self.t = t
        self.lw = None
        self.rd = []
        self.dsem = None
        self.dcnt = 0
        self.psum = False

    def __getitem__(self, idx):
        return self.t[idx]


class KB:
    def __init__(self, nc, stack):
        self.nc = nc
        self.stack = stack
        self.ops = {e: [] for e in ENGS}
        self.cnt = {e: 0 for e in ENGS}
        self.esems = {e: [] for e in ENGS}
        self.waited = {e: {} for e in ENGS}
        self.nbuf = 0
        self.maxops = None
        self.rec = None
        self.nops = 0
        self.final = False

    def sb(self, name, shape, dt=F32):
        t = self.stack.enter_context(self.nc.sbuf_tensor(name, list(shape), dt))
        return Buf(name, t)

    def ps(self, name, shape, dt=F32):
        t = self.stack.enter_context(self.nc.psum_tensor(name, list(shape), dt))
        b = Buf(name, t)
        b.psum = True
        return b

    def dram(self, name, shape, dt=F32, kind="ExternalInput"):
        t = self.nc.dram_tensor(name, list(shape), dt, kind=kind)
        return Buf(name, t.ap())

    def _esem(self, eng, idx):
        ep = (idx - 1) // SEM_EPOCH
        lst = self.esems[eng]
        while len(lst) <= ep:
            lst.append(self.stack.enter_context(self.nc.semaphore(f"s_{eng}_{len(lst)}")))
        return lst[ep], idx - ep * SEM_EPOCH

    def _deps(self, reads, writes):
        deps = []
        for b in list(reads) + list(writes):
            if b.lw is not None:
                deps.append(b.lw)
        for b in writes:
            deps.extend(b.rd)
        for b in reads:
            if b.psum:
                deps.extend(b.rd)
        return deps

    def _mark(self, tok, reads, writes):
        for b in reads:
            b.rd.append(tok)
        for b in writes:
            b.lw = tok
            b.rd = []

    def _waits(self, eng, deps):
        wd = self.waited[eng]
        best = {}
        for d in deps:
            if d[0] == "e":
                _, e2, i2 = d
                if e2 == eng and eng in ("pe", "sp"):
                    continue
                key = ("e", e2)
                if i2 > best.get(key, (0, None))[0]:
                    best[key] = (i2, None)
            else:
                _, b, v = d
                v = b.dcnt
                key = ("d", b.name)
                if v > best.get(key, (0, None))[0]:
                    best[key] = (v, b)
        waits = []
        for key, (v, b) in best.items():
            if wd.get(key, 0) >= v:
                continue
            wd[key] = v
            if key[0] == "e":
                waits.append(self._esem(key[1], v))
            else:
                waits.append((b.dsem, v))
        return waits

    def op(self, eng, fn, reads=(), writes=()):
        if self.rec is not None:
            self.rec.append(("op", (eng, fn, tuple(reads), tuple(writes)), {}))
            return
        self.nops += 1
        if self.maxops is not None and self.nops > self.maxops:
            return
        deps = self._deps(reads, writes)
        waits = self._waits(eng, deps)
        self.cnt[eng] += 1
        idx = self.cnt[eng]
        self.ops[eng].append((waits, fn, (self._esem(eng, idx)[0], 1)))
        self._mark(("e", eng, idx), reads, writes)

    def dma(self, out_ap, in_ap, reads, writes, sembuf, eng="sp", **kw):
        if self.rec is not None:
            self.rec.append(("dma", (out_ap, in_ap, tuple(reads), tuple(writes), sembuf), dict(eng=eng, **kw)))
            return
        self.nops += 1
        if self.maxops is not None and self.nops > self.maxops:
            return
        deps = self._deps(reads, writes)
        waits = self._waits(eng, deps)
        if sembuf.dsem is None:
            sembuf.dsem = self.stack.enter_context(self.nc.semaphore(f"d_{sembuf.name}"))
        sembuf.dcnt += 16
        v = sembuf.dcnt
        sem = sembuf.dsem

        def fn(e):
            return e.dma_start(out=out_ap, in_=in_ap, **kw)
        self.ops[eng].append((waits, fn, (sem, 16)))
        self._mark(("d", sembuf, v), reads, writes)

    def play(self, streams):
        idx = [0] * len(streams)
        live = True
        while live:
            live = False
            for si, st_ in enumerate(streams):
                if idx[si] < len(st_):
                    kind, a, kw = st_[idx[si]]
                    idx[si] += 1
                    live = True
                    if kind == "op":
                        self.op(*a)
                    else:
                        self.dma(*a, **kw)

    def final_wait(self, eng, bufs):
        waits = []
        for b in bufs:
            if b.dsem is not None:
                waits.append((b.dsem, b.dcnt))
        self.ops[eng].append((waits, None, None))

    def emit(self):
        nc = self.nc
        with nc.Block() as block:
            def run(e, lst, is_dma_ok=True):
                for waits, fn, inc in lst:
                    for (s, v) in waits:
                        e.wait_ge(s, v)
                    if fn is None:
                        continue
                    ins = fn(e)
                    if inc is not None:
                        ins.then_inc(inc[0], inc[1])

            @block.tensor
            def _(e):
                run(e, self.ops["pe"])

            @block.scalar
            def _(e):
                run(e, self.ops["act"])

            @block.vector
            def _(e):
                run(e, self.ops["dve"])

            @block.gpsimd
            def _(e):
                run(e, self.ops["pool"])

            @block.sync
            def _(e):
                run(e, self.ops["sp"])


DEPTH = 4
NT = 17
TOK0 = [0] + [16 + 128 * j for j in range(16)]
TLEN = [16] + [128] * 16
STOK = 2064
EPS = 1e-6
NLAYERS_BUILD = DEPTH
INV_DT = F32


def build_program(n_layers=DEPTH, do_samples=True, max_phases=None):
    nc = bass.Bass("TRN2", target_bir_lowering=False)
    st = ExitStack()
    with st:
        k = KB(nc, st)
        import os
        if os.environ.get('KMAXOPS'):
            k.maxops = int(os.environ['KMAXOPS'])
        D = {}
        def din(name, shape):
            D[name] = k.dram(name, shape)
            return D[name]
        def dout(name, shape):
            D[name] = k.dram(name, shape, kind="ExternalOutput")
            return D[name]
        xp = din("xp", [2064, 1024]); xsm = din("xsm", [16, 1024])
        WM = din("WM", [DEPTH, 4, 128, 8, 1026]); WG = din("WG", [DEPTH, 8, 128, 8, 514])
        WO = din("WO", [DEPTH, 16, 128, 1024])
        sC = din("sC", [DEPTH, 16, 4, 128, 256]); sn = din("sn", [DEPTH, 16, 4, 128]); sm = din("sm", [DEPTH, 16, 4])
        sS = din("sS", [DEPTH, 16, 8, 128, 128]); sconv = din("sconv", [DEPTH, 16, 3, 3072])
        cident = din("cident", [128, 128]); cutincl = din("cutincl", [128, 128]); cutstr = din("cutstr", [128, 128])
        cltstr = din("cltstr", [128, 128]); ceye16 = din("ceye16", [128, 16, 16])
        nwT = din("nwT", [128, DEPTH * 8]); wmT = din("wmT", [128, DEPTH * 8]); wgT = din("wgT", [128, DEPTH])
        cwT = din("cwT", [128, DEPTH * 24 * 4]); gpb = din("gpb", [128, DEPTH * 24]); fnw = din("fnw", [128, 1024])
        yp = dout("yp", [2048, 1024]); ys = dout("ys", [16, 1024])
        pC = dout("pC", [DEPTH, 4, 128, 256]); pn = dout("pn", [DEPTH, 4, 128]); pm = dout("pm", [DEPTH, 4])
        pS = dout("pS", [DEPTH, 8, 128, 128]); pconv = dout("pconv", [DEPTH, 3, 3072])
        oC = dout("oC", [DEPTH, 16, 4, 128, 256]); on = dout("on", [DEPTH, 16, 4, 128]); om = dout("om", [DEPTH, 16, 4])
        oS = dout("oS", [DEPTH, 16, 8, 128, 128]); oconv = dout("oconv", [DEPTH, 16, 3, 3072])

        class NS:
            pass
        xt = [k.sb(f"x{t}", [128, 1024]) for t in range(NT + 1)]
        hT_t = st.enter_context(nc.sbuf_tensor("hT", [128, 8, 2080], BF16))
        hT = [Buf(f"hT{t}", hT_t) for t in range(NT + 1)]
        ident = k.sb("ident", [128, 128]); utincl = k.sb("utincl", [128, 128]); utstr = k.sb("utstr", [128, 128])
        ltstr = k.sb("ltstr", [128, 128]); ones = k.sb("ones", [128, 128]); eye16 = k.sb("eye16", [128, 16, 16])
        identb = k.sb("identb", [128, 128], BF16); onesb = k.sb("onesb", [128, 128], BF16)
        nw = k.sb("nw", [128, DEPTH * 8]); wm = k.sb("wm", [128, DEPTH * 8]); wg = k.sb("wg", [128, DEPTH])
        cw = k.sb("cw", [128, DEPTH * 96]); gp = k.sb("gp", [128, DEPTH * 24]); gpd = k.sb("gpd", [128, DEPTH * 24])
        xn = k.sb("xn", [128, 1024], BF16)
        LANE_NAMES = ("Wb Wob g gb mb junk ctmp dg ktok Vp qT qTd kT QKm E G ym yT Cext Cbf Sst Sbf xc cs cacc sq rs vTb vtok "
                      "gl GamT t1 Um Nm Pm Rp QKG vnew kd osb PA PB PC PT").split()
        def mk_lane(i):
            ln = NS()
            sb = lambda name, shape, dt=F32: k.sb(f"{name}_{i}", shape, dt)
            ln.Wb = sb("Wb", [128, 8, 1026], BF16); ln.Wob = sb("Wob", [128, 2, 1024], BF16)
            ln.g = sb("g", [128, 40]); ln.gb = sb("gb", [128, 16]); ln.mb = sb("mb", [128, 1])
            ln.junk = sb("junk", [128, 256], BF16); ln.ctmp = sb("ctmp", [128, 9]); ln.dg = sb("dg", [128, 128])
            ln.ktok = sb("ktok", [128, 128], BF16); ln.Vp = sb("Vp", [128, 258], BF16)
            ln.qT = sb("qT", [128, 128], BF16); ln.qTd = sb("qTd", [128, 128], BF16); ln.kT = sb("kT", [128, 128], BF16)
            ln.QKm = sb("QKm", [128, 128], BF16); ln.E = sb("E", [128, 512]); ln.G = sb("G", [128, 256])
            ln.ym = sb("ym", [128, 256], BF16); ln.yT = sb("yT", [128, 2, 128], BF16)
            ln.Cext = sb("Cext", [128, 257]); ln.Cbf = sb("Cbf", [128, 257], BF16)
            ln.Sst = ln.Cext; ln.Sbf = ln.Cbf
            ln.xc = sb("xc", [128, 3, 131]); ln.cs = sb("cs", [128, 3, 128]); ln.cacc = sb("cacc", [128, 3, 128])
            ln.sq = sb("sq", [128, 256], BF16); ln.rs = sb("rs", [128, 256])
            ln.vTb = sb("vTb", [128, 128], BF16); ln.vtok = ln.Vp
            ln.gl = sb("gl", [128, 128]); ln.GamT = sb("GamT", [128, 128]); ln.t1 = ln.dg
            ln.Um = [sb(f"Um{j}", [128, 128], INV_DT) for j in range(2)]
            ln.Nm = [sb(f"Nm{j}", [128, 128], INV_DT) for j in range(2)]
            ln.Pm = sb("Pm", [128, 128], INV_DT); ln.Rp = sb("Rp", [128, 128], INV_DT)
            ln.QKG = ln.QKm; ln.vnew = sb("vnew", [128, 128], BF16); ln.kd = ln.qTd
            ln.osb = sb("osb", [128, 128])
            ln.PA = k.ps(f"PA_{i}", [128, 512]); ln.PB = k.ps(f"PB_{i}", [128, 512]); ln.PC = k.ps(f"PC_{i}", [128, 512])
            ln.PT = k.ps(f"PT_{i}", [128, 8, 128], BF16)
            return ln
        lanes = [mk_lane(0), mk_lane(1)]
        Cs = k.sb("Cs", [128, 2056])
        CsM = Cs[:, :].rearrange("p (s e) -> p s e", e=257)
        CsG = Cs[:, 0:2048].rearrange("p (s e) -> p s e", e=128)
        CsB = [Buf(f"Cs_s{j}", Cs.t) for j in range(16)]
        msm = k.sb("msm", [16, 4]); hist = k.sb("hist", [16, 3, 128]); histT = k.sb("histT", [128, 9, 16])
        qTf = k.sb("qTf", [128, 16]); kTf = k.sb("kTf", [128, 16])
        QTm = k.sb("QTm", [128, 16, 16]); KTm = k.sb("KTm", [128, 16, 16])
        Kp = k.sb("Kp", [16, 128], BF16); Kmask2 = [k.sb(f"Kmask{j}", [16, 128], BF16) for j in range(2)]; Vx = k.sb("Vx", [16, 258], BF16)
        decb = k.sb("decb", [128, 16]); dg16 = k.sb("dg16", [16, 16])
        pfT = k.sb("pfT", [128, 3, 16])

        def mm(out, lhsT, rhs, R, W, start=True, stop=True):
            k.op("pe", lambda e: e.matmul(out=out, lhsT=lhsT, rhs=rhs, start=start, stop=stop), R, W)
        def tr(out, in_, idn, R, W):
            k.op("pe", lambda e: e.transpose(out=out, in_=in_, identity=idn), R, W)
        def act(out, in_, func, R, W, bias=0.0, scale=1.0, accum=None, eng="act"):
            if accum is None:
                k.op("act", lambda e: e.activation(out=out, in_=in_, func=func, bias=bias, scale=scale), R, W)
            else:
                k.op("act", lambda e: e.activation(out=out, in_=in_, func=func, bias=bias, scale=scale, accum_out=accum), R, W)
        def tt(eng, out, in0, in1, op, R, W):
            k.op(eng, lambda e: e.tensor_tensor(out=out, in0=in0, in1=in1, op=op), R, W)
        def ts(eng, out, in0, s1, op0, R, W, s2=None, op1=None):
            if op1 is None:
                k.op(eng, lambda e: e.tensor_scalar(out=out, in0=in0, scalar1=s1, scalar2=None, op0=op0), R, W)
            else:
                k.op(eng, lambda e: e.tensor_scalar(out=out, in0=in0, scalar1=s1, scalar2=s2, op0=op0, op1=op1), R, W)
        def stt(eng, out, in0, sc, in1, op0, op1, R, W):
            k.op(eng, lambda e: e.scalar_tensor_tensor(out=out, in0=in0, scalar=sc, in1=in1, op0=op0, op1=op1), R, W)
        def cp(eng, out, in_, R, W):
            k.op(eng, lambda e: e.tensor_copy(out=out, in_=in_), R, W)
        def recip(out, in_, R, W):
            k.op("dve", lambda e: e.reciprocal(out=out, in_=in_), R, W)
        def ld(out, in_, dbuf, sbuf, eng="sp", **kw):
            k.dma(out, in_, [dbuf], [sbuf], sbuf, eng=eng, **kw)
        def stv(out, in_, sbuf, dbuf, **kw):
            k.dma(out, in_, [sbuf], [dbuf], sbuf, **kw)
        MUL, ADD, SUB, MAX = ALU.mult, ALU.add, ALU.subtract, ALU.max

        ld(ident[:], cident[:], cident, ident); ld(utincl[:], cutincl[:], cutincl, utincl)
        ld(utstr[:], cutstr[:], cutstr, utstr); ld(ltstr[:], cltstr[:], cltstr, ltstr)
        ld(eye16[:], ceye16[:], ceye16, eye16)
        ld(nw[:], nwT[:], nwT, nw); ld(wm[:], wmT[:], wmT, wm); ld(wg[:], wgT[:], wgT, wg)
        ld(cw[:], cwT[:], cwT, cw); ld(gp[:], gpb[:], gpb, gp)
        k.op("pool", lambda e: e.memset(ones[:], 1.0), [], [ones])
        k.op("pool", lambda e: e.memset(onesb[:], 1.0), [], [onesb])
        cp("dve", identb[:], ident[:], [ident], [identb])
        cp("dve", gpd[:], gp[:], [gp], [gpd])
        for l in range(DEPTH):
            ts("dve", gpd[:, l * 24 + 4:l * 24 + 8], gp[:, l * 24 + 4:l * 24 + 8], -1.0, MUL, [gp], [gpd])
            act(gpd[:, l * 24 + 8:l * 24 + 16], gp[:, l * 24 + 8:l * 24 + 16], AF.Exp, [gp], [gpd])
            ts("dve", gpd[:, l * 24 + 8:l * 24 + 16], gpd[:, l * 24 + 8:l * 24 + 16], -1.0, MUL, [gpd], [gpd])
        for t in range(NT):
            L = TLEN[t]
            ld(xt[t][0:L, :], xp[TOK0[t]:TOK0[t] + L, :], xp, xt[t])
        ld(xt[NT][0:16, :], xsm[:, :], xsm, xt[NT])

        all_tiles = list(range(NT)) + ([NT] if do_samples else [])
        def tinfo(t):
            if t == NT:
                return 16, STOK
            return TLEN[t], TOK0[t]

        def load_weights(ln, l, kind, h):
            xb = []
            if kind == "m":
                k.dma(ln.Wb[:, :, 0:1026], WM[l, h], [WM], [ln.Wb] + xb, ln.Wb, eng="pool")
                k.dma(ln.Wob[:, 0:2, :], WO[l, 2 * h:2 * h + 2].rearrange("j p c -> p j c"), [WO], [ln.Wob], ln.Wob, eng="pool")
            else:
                k.dma(ln.Wb[:, :, 0:514], WG[l, h], [WG], [ln.Wb] + xb, ln.Wb, eng="pool")
                k.dma(ln.Wob[:, 0:1, :], WO[l, 8 + h:9 + h].rearrange("j p c -> p j c"), [WO], [ln.Wob], ln.Wob, eng="pool")

        phases = [(l, kind, h) for l in range(n_layers) for (kind, nh) in (("m", 4), ("g", 8)) for h in range(nh)]
        if max_phases is not None:
            phases = phases[:max_phases]

        def prologue(l):
            ln = lanes[0]
            g = ln.g; T0 = ln.PT; junk = xn
            for t in all_tiles:
                L, tok = tinfo(t)
                act(xn[0:L, :], xt[t][0:L, :], AF.Square, [xt[t]], [xn, g], accum=g[0:L, 0:1])
                act(g[0:L, 1:2], g[0:L, 0:1], AF.Ln, [g], [g], bias=EPS, scale=1.0 / 1024.0)
                act(g[0:L, 1:2], g[0:L, 1:2], AF.Exp, [g], [g], scale=-0.5)
                ts("dve", xn[0:L, :], xt[t][0:L, :], g[0:L, 1:2], MUL, [xt[t], g], [xn])
                for kt in range(8):
                    tr(T0[:, kt, 0:L], xn[0:L, kt * 128:(kt + 1) * 128], identb[0:L, 0:L], [xn, identb], [T0])
                tt("dve", hT_t[:, :, tok:tok + L], T0[:, :, 0:L],
                   nw[:, l * 8:(l + 1) * 8].rearrange("p (k o) -> p k o", o=1).to_broadcast([128, 8, L]),
                   MUL, [T0, nw], [hT[t]])

        def run_phase(ln, l, kind, h, tiles):
            (Wb_, Wob_, g, gb, mb, junk, ctmp, dg, ktok, Vp, qT, qTd, kT, QKm, E, G, ym, yT, Cext, Cbf, Sst, Sbf, xc, cs, cacc,
             sq, rs, vTb, vtok, gl, GamT, t1, Um, Nm, Pm, Rp, QKG, vnew, kd, osb, PA, PB, PC, PT) = [getattr(ln, n) for n in LANE_NAMES]
            W = Wb_; Wo = Wob_
            gq = l * 24
            sck = 128.0 ** -0.5

            def rstd_from_ss(L, col_ss, col_out, n):
                act(g[0:L, col_out:col_out + 1], g[0:L, col_ss:col_ss + 1], AF.Ln, [g], [g], bias=EPS, scale=1.0 / n)
                act(g[0:L, col_out:col_out + 1], g[0:L, col_out:col_out + 1], AF.Exp, [g], [g], scale=-0.5)

            def out_proj(t, L, nj, wbuf, wcols):
                for j in range(nj):
                    tr(PT[:, 4 + j, 0:L], ym[0:L, j * 128:(j + 1) * 128], identb[0:L, 0:L], [ym, identb], [PT])
                for j in range(nj):
                    ts("dve", yT[:, j, 0:L], PT[:, 4 + j, 0:L], wcols[j], MUL, [PT, wbuf], [yT])
                for nb, PX in ((0, PA), (1, PB)):
                    for j in range(nj):
                        mm(PX[0:L, 0:512], yT[:, j, 0:L], Wo[:, j, nb * 512:(nb + 1) * 512],
                           [yT, Wo], [PX], start=(j == 0), stop=(j == nj - 1))
                    tt("dve", xt[t][0:L, nb * 512:(nb + 1) * 512], xt[t][0:L, nb * 512:(nb + 1) * 512], PX[0:L, 0:512], ADD, [xt[t], PX], [xt[t]])

            first = (tiles[0] == 0)
            last_prompt = (NT - 1 in tiles)
            if kind == "m":
                if first:
                    k.op("pool", lambda e: e.memset(Cext[:], 0.0), [], [Cext])
                    k.op("pool", lambda e: e.memset(Cbf[:], 0.0), [], [Cbf])
                    k.op("pool", lambda e: e.memset(mb[:], 0.0), [], [mb])
                for t in tiles:
                    L, tok = tinfo(t)
                    smp = (t == NT)
                    if smp:
                        ld(msm[:, :], sm[l, :, :], sm, msm)
                    for kt in range(8):
                        mm(PA[0:L, 0:384], hT_t[:, kt, tok:tok + L], W[:, kt, 128:512], [hT[t], W], [PA], start=(kt == 0), stop=(kt == 7))
                    for kt in range(8):
                        mm(PA[0:L, 384:386], hT_t[:, kt, tok:tok + L], W[:, kt, 1024:1026], [hT[t], W], [PA], start=(kt == 0), stop=(kt == 7))
                    for kt in range(8):
                        mm(PB[0:L, 0:512], hT_t[:, kt, tok:tok + L], W[:, kt, 512:1024], [hT[t], W], [PB], start=(kt == 0), stop=(kt == 7))
                    for j in range(2):
                        for kt in range(8):
                            mm(PC[:, j * L:(j + 1) * L], W[:, kt, j * 128:(j + 1) * 128], hT_t[:, kt, tok:tok + L], [hT[t], W], [PC], start=(kt == 0), stop=(kt == 7))
                    act(E[0:L, :], PB[0:L, 0:512], AF.Exp, [PB], [E], scale=-1.0)
                    ts("dve", E[0:L, :], E[0:L, :], 1.0, ADD, [E], [E])
                    tt("dve", E[0:L, 0:256], E[0:L, 0:256], E[0:L, 256:512], MUL, [E], [E])
                    recip(E[0:L, 0:256], E[0:L, 0:256], [E], [E])
                    tt("dve", G[0:L, :], E[0:L, 0:256], PB[0:L, 256:512], MUL, [E, PB], [G])
                    act(g[0:L, 0:2], PA[0:L, 384:386], AF.Copy, [PA], [g])
                    ts("dve", g[0:L, 2:3], g[0:L, 0:1], gpd[0:L, gq + h:gq + h + 1], ADD, [g, gpd], [g])
                    act(g[0:L, 3:4], g[0:L, 1:2], AF.Exp, [g, gpd], [g], bias=gpd[0:L, gq + 4 + h:gq + 5 + h], scale=-1.0)
                    act(g[0:L, 4:5], g[0:L, 3:4], AF.Ln, [g], [g], bias=1.0)
                    ts("dve", g[0:L, 5:6], g[0:L, 4:5], -1.0, MUL, [g], [g])
                    if not smp:
                        mm(PC[0:L, 384:385], utincl[0:L, 0:L], g[0:L, 5:6], [utincl, g], [PC])
                        mm(PC[:, 385:386], ones[0:L, :], g[0:L, 5:6], [ones, g], [PC])
                        tt("dve", g[0:L, 6:7], g[0:L, 2:3], PC[0:L, 384:385], SUB, [g, PC], [g])
                        ts("dve", dg[0:L, 0:L], ident[0:L, 0:L], g[0:L, 6:7], MUL, [ident, g], [dg])
                        mm(PC[:, 256:256 + L], ones[0:L, :], dg[0:L, 0:L], [ones, dg], [PC])
                        k.op("dve", lambda e, L=L: e.tensor_reduce(out=gb[:, 0:1], in_=PC[:, 256:256 + L], axis=AX.X, op=MAX), [PC], [gb])
                        tt("dve", gb[:, 1:2], gb[:, 0:1], mb[:, 0:1], MAX, [gb, mb], [gb])
                        ts("dve", gb[:, 2:3], gb[:, 1:2], -1.0, MUL, [gb], [gb])
                        act(gb[:, 3:4], mb[:, 0:1], AF.Exp, [mb, gb], [gb], bias=gb[:, 2:3])
                        act(g[0:L, 7:8], g[0:L, 6:7], AF.Exp, [g, gb], [g], bias=gb[0:L, 2:3])
                        tt("dve", gb[:, 4:5], PC[:, 385:386], gb[:, 1:2], ADD, [PC, gb], [gb])
                        act(g[0:L, 8:9], PC[0:L, 384:385], AF.Exp, [PC, gb], [g], bias=gb[0:L, 2:3], scale=-1.0)
                    else:
                        tt("dve", g[0:L, 6:7], g[0:L, 5:6], msm[0:16, h:h + 1], ADD, [g, msm], [g])
                        tt("dve", g[0:L, 20:21], g[0:L, 6:7], g[0:L, 2:3], MAX, [g], [g])
                        ts("dve", g[0:L, 21:22], g[0:L, 20:21], -1.0, MUL, [g], [g])
                        act(g[0:L, 22:23], g[0:L, 6:7], AF.Exp, [g], [g], bias=g[0:L, 21:22])
                        act(g[0:L, 7:8], g[0:L, 2:3], AF.Exp, [g], [g], bias=g[0:L, 21:22])
                        act(g[0:L, 8:9], g[0:L, 20:21], AF.Exp, [g], [g], scale=-1.0)
                        stv(om[l, :, h:h + 1], g[0:16, 20:21], g, om, allow_slow_non_contiguous=True)
                    act(qT[:, 0:L], PC[:, 0:L], AF.Copy, [PC], [qT])
                    ts("dve", kT[:, 0:L], PC[:, L:2 * L], sck, MUL, [PC], [kT])
                    if not smp:
                        ts("dve", ktok[0:L, :], PA[0:L, 0:128], sck, MUL, [PA], [ktok])
                        ts("dve", Vp[0:L, 0:256], PA[0:L, 128:384], g[0:L, 7:8], MUL, [PA, g], [Vp])
                        cp("dve", Vp[0:L, 256:258], g[0:L, 7:8].to_broadcast([L, 2]), [g], [Vp])
                        ts("dve", qTd[:, 0:L], PC[:, 0:L], gb[:, 3:4], MUL, [PC, gb], [qTd])
                        mm(PC[0:L, 256:256 + L], kT[:, 0:L], qT[:, 0:L], [kT, qT], [PC])
                        tt("dve", QKm[0:L, 0:L], PC[0:L, 256:256 + L], utincl[0:L, 0:L], MUL, [PC, utincl], [QKm])
                        mm(PA[0:L, 0:257], qTd[:, 0:L], Cbf[:, :], [qTd, Cbf], [PA], start=True, stop=False)
                        mm(PA[0:L, 0:257], QKm[0:L, 0:L], Vp[0:L, 0:257], [QKm, Vp], [PA], start=False, stop=True)
                    else:
                        ts("dve", Kp[0:16, :], PA[0:16, 0:128], g[0:16, 7:8], MUL, [PA, g], [Kp], s2=sck, op1=MUL)
                        cp("dve", Vx[0:16, 0:256], PA[0:16, 128:384], [PA], [Vx])
                        k.op("pool", lambda e: e.memset(Vx[0:16, 256:258], 1.0), [], [Vx])
                        ts("dve", dg16[:, :], ident[0:16, 0:16], g[0:16, 22:23], MUL, [ident, g], [dg16])
                        mm(PC[:, 256:272], ones[0:16, :], dg16[:, :], [ones, dg16], [PC])
                        cp("dve", decb[:, :], PC[:, 256:272], [PC], [decb])
                        cp("dve", qTf[:, :], PC[:, 0:16], [PC], [qTf])
                        tt("dve", QTm[:, :, :], eye16[:, :, :], qTf[:, 0:16].rearrange("p (o s) -> p o s", o=1).to_broadcast([128, 16, 16]), MUL, [eye16, qTf], [QTm])
                        for hf in range(2):
                            s0 = 8 * hf
                            k.dma(CsM[:, :, 0:256], sC[l, s0:s0 + 8, h].rearrange("s d e -> d s e"), [sC], [Cs] + CsB[0:8], Cs)
                            k.dma(CsM[:, :, 256:257], sn[l, s0:s0 + 8, h, :].rearrange("s (d o) -> d s o", o=1), [sn], [Cs] + CsB[0:8], Cs, allow_slow_non_contiguous=True)
                            for i8 in range(8):
                                i = s0 + i8
                                Kmask = Kmask2[i % 2]
                                PX, pxa = (PB, PB[:, 0:257]) if i % 2 == 0 else (PC, PC[:, 0:257])
                                ts("dve", Kmask[:, :], Kp[:, :], ident[0:16, i:i + 1], MUL, [Kp, ident], [Kmask])
                                mm(pxa, Kmask[:, :], Vx[:, 0:257], [Kmask, Vx], [PX])
                                stt("dve", CsM[:, i8, :], CsM[:, i8, :], decb[:, i:i + 1], pxa, MUL, ADD, [CsB[i8], decb, PX], [CsB[i8]])
                                mm(PA[0:16, 0:257], QTm[:, i, :], CsM[:, i8, :], [QTm, CsB[i8]], [PA], start=(i == 0), stop=(i == 15))
                            k.dma(oC[l, s0:s0 + 8, h].rearrange("s d e -> d s e"), CsM[:, :, 0:256], [Cs] + CsB[0:8], [oC], Cs)
                            k.dma(on[l, s0:s0 + 8, h, :].rearrange("s (d o) -> d s o", o=1), CsM[:, :, 256:257], [Cs] + CsB[0:8], [on], Cs, allow_slow_non_contiguous=True)
                    ts("dve", g[0:L, 9:10], PA[0:L, 256:257], -1.0, MUL, [PA], [g])
                    tt("dve", g[0:L, 9:10], g[0:L, 9:10], PA[0:L, 256:257], MAX, [g, PA], [g])
                    tt("dve", g[0:L, 10:11], g[0:L, 9:10], g[0:L, 8:9], MAX, [g], [g])
                    recip(g[0:L, 11:12], g[0:L, 10:11], [g], [g])
                    act(junk[0:L, 0:256], PA[0:L, 0:256], AF.Square, [PA], [junk, g], accum=g[0:L, 12:13])
                    stt("dve", g[0:L, 13:14], g[0:L, 12:13], g[0:L, 11:12], g[0:L, 11:12], MUL, MUL, [g], [g])
                    rstd_from_ss(L, 13, 15, 256.0)
                    tt("dve", g[0:L, 16:17], g[0:L, 15:16], g[0:L, 11:12], MUL, [g], [g])
                    stt("dve", ym[0:L, 0:256], PA[0:L, 0:256], g[0:L, 16:17], G[0:L, :], MUL, MUL, [PA, g, G], [ym])
                    if not smp:
                        mm(PB[:, 0:257], ktok[0:L, :], Vp[0:L, 0:257], [ktok, Vp], [PB])
                        stt("dve", Cext[:, :], Cext[:, :], gb[:, 3:4], PB[:, 0:257], MUL, ADD, [Cext, gb, PB], [Cext])
                        act(Cbf[:, :], Cext[:, :], AF.Copy, [Cext], [Cbf])
                        cp("dve", mb[:, :], gb[:, 4:5], [gb], [mb])
                    out_proj(t, L, 2, wm, [wm[:, l * 8 + 2 * h + j:l * 8 + 2 * h + j + 1] for j in range(2)])
                if last_prompt:
                    stv(pC[l, h], Cext[:, 0:256], Cext, pC)
                    stv(pn[l, h:h + 1, :].rearrange("o d -> d o"), Cext[:, 256:257], Cext, pn, allow_slow_non_contiguous=True)
                    stv(pm[l:l + 1, h:h + 1], mb[0:1, 0:1], mb, pm)
            else:
                if first:
                    k.op("pool", lambda e: e.memset(Sst[:], 0.0), [], [Sst])
                    k.op("pool", lambda e: e.memset(Sbf[:], 0.0), [], [Sbf])
                    k.op("pool", lambda e: e.memset(xc[:], 0.0), [], [xc])
                cwq = l * 96
                for t in tiles:
                    L, tok = tinfo(t)
                    smp = (t == NT)
                    if smp:
                        k.dma(CsG[:, :, :], sS[l, :, h].rearrange("s d e -> d s e"), [sS], [Cs] + CsB, Cs)
                        if h == 0:
                            k.dma(oconv[l, :, 0:2, :], sconv[l, :, 1:3, :], [sconv], [oconv], oconv)
                    for p in range(3):
                        for kt in range(8):
                            mm(PA[:, p * 128:p * 128 + L], W[:, kt, p * 128:(p + 1) * 128], hT_t[:, kt, tok:tok + L], [hT[t], W], [PA], start=(kt == 0), stop=(kt == 7))
                    for kt in range(8):
                        mm(PB[0:L, 0:130], hT_t[:, kt, tok:tok + L], W[:, kt, 384:514], [hT[t], W], [PB], start=(kt == 0), stop=(kt == 7))
                    if not smp:
                        for p in range(3):
                            act(xc[:, p, 3:3 + L], PA[:, p * 128:p * 128 + L], AF.Copy, [PA], [xc])
                        taps = [[xc[:, p, j:j + L] for j in range(4)] for p in range(3)]
                        creads = [xc]
                    else:
                        for p in range(3):
                            ld(hist[:, :, :], sconv[l, :, :, p * 1024 + h * 128:p * 1024 + (h + 1) * 128], sconv, hist)
                            for j in range(3):
                                tr(PC[:, (p * 3 + j) * 16:(p * 3 + j + 1) * 16], hist[0:16, j, :], ident[0:16, 0:16], [hist, ident], [PC])
                        cp("dve", histT[:, :, :], PC[:, 0:144].rearrange("p (j s) -> p j s", s=16), [PC], [histT])
                        for p in range(3):
                            act(pfT[:, p, :], PA[:, p * 128:p * 128 + 16], AF.Copy, [PA], [pfT])
                        taps = [[histT[:, p * 3 + j, :] for j in range(3)] + [pfT[:, p, :]] for p in range(3)]
                        creads = [histT, pfT]
                        for p in range(3):
                            stv(oconv[l, :, 2, p * 1024 + h * 128:p * 1024 + (h + 1) * 128].rearrange("s c -> c s"), pfT[:, p, :], pfT, oconv, allow_slow_non_contiguous=True)
                    for p in range(3):
                        wc = lambda j, p=p: cw[:, cwq + (p * 8 + h) * 4 + j:cwq + (p * 8 + h) * 4 + j + 1]
                        ts("dve", cacc[:, p, 0:L], taps[p][0], wc(0), MUL, creads + [cw], [cacc])
                        for j in range(1, 4):
                            stt("dve", cacc[:, p, 0:L], taps[p][j], wc(j), cacc[:, p, 0:L], MUL, ADD, creads + [cw, cacc], [cacc])
                    if not smp:
                        if t == NT - 1:
                            for p in range(3):
                                stv(pconv[l, :, p * 1024 + h * 128:p * 1024 + (h + 1) * 128].rearrange("j c -> c j"), xc[:, p, L:L + 3], xc, pconv, allow_slow_non_contiguous=True)
                        cp("dve", ctmp[:, 0:9].rearrange("c (p j) -> c p j", j=3), xc[:, :, L:L + 3], [xc], [ctmp])
                        cp("dve", xc[:, :, 0:3], ctmp[:, 0:9].rearrange("c (p j) -> c p j", j=3), [ctmp], [xc])
                    act(cs[:, :, 0:L], cacc[:, :, 0:L], AF.Exp, [cacc], [cs], scale=-1.0)
                    ts("dve", cs[:, :, 0:L], cs[:, :, 0:L], 1.0, ADD, [cs], [cs])
                    recip(cs[:, :, 0:L], cs[:, :, 0:L], [cs], [cs])
                    tt("dve", cs[:, :, 0:L], cs[:, :, 0:L], cacc[:, :, 0:L], MUL, [cs, cacc], [cs])
                    tt("dve", sq[:, 0:2 * L].rearrange("c (p t) -> c p t", p=2), cs[:, 0:2, 0:L], cs[:, 0:2, 0:L], MUL, [cs], [sq])
                    mm(PB[:, 256:256 + 2 * L], onesb[:, :], sq[:, 0:2 * L], [onesb, sq], [PB])
                    act(rs[:, 0:2 * L], PB[:, 256:256 + 2 * L], AF.Ln, [PB], [rs], bias=EPS)
                    act(rs[:, 0:2 * L], rs[:, 0:2 * L], AF.Exp, [rs], [rs], scale=-0.5)
                    stt("dve", qT[:, 0:L], cs[:, 0, 0:L], 128.0 ** -0.5, rs[:, 0:L], MUL, MUL, [cs, rs], [qT])
                    tt("dve", kT[:, 0:L], cs[:, 1, 0:L], rs[:, L:2 * L], MUL, [cs, rs], [kT])
                    act(vTb[:, 0:L], cs[:, 2, 0:L], AF.Copy, [cs], [vTb])
                    tr(PT[0:L, 0, :], kT[:, 0:L], identb[:, :], [kT, identb], [PT])
                    tr(PT[0:L, 1, :], vTb[:, 0:L], identb[:, :], [vTb, identb], [PT])
                    cp("dve", ktok[0:L, :], PT[0:L, 0, :], [PT], [ktok])
                    cp("dve", vtok[0:L, 0:128], PT[0:L, 1, :], [PT], [vtok])
                    act(g[0:L, 0:1], PB[0:L, 128:129], AF.Exp, [PB], [g], scale=-1.0)
                    ts("dve", g[0:L, 0:1], g[0:L, 0:1], 1.0, ADD, [g], [g])
                    recip(g[0:L, 1:2], g[0:L, 0:1], [g], [g])
                    act(g[0:L, 2:3], PB[0:L, 129:130], AF.Exp, [PB, gpd], [g], bias=gpd[0:L, gq + 16 + h:gq + 17 + h])
                    act(g[0:L, 3:4], g[0:L, 2:3], AF.Ln, [g], [g], bias=1.0)
                    ts("dve", g[0:L, 5:6], g[0:L, 3:4], gpd[0:L, gq + 8 + h:gq + 9 + h], MUL, [g, gpd], [g])
                    act(E[0:L, 0:128], PB[0:L, 0:128], AF.Exp, [PB], [E], scale=-1.0)
                    ts("dve", E[0:L, 0:128], E[0:L, 0:128], 1.0, ADD, [E], [E])
                    recip(E[0:L, 0:128], E[0:L, 0:128], [E], [E])
                    tt("dve", G[0:L, 0:128], E[0:L, 0:128], PB[0:L, 0:128], MUL, [E, PB], [G])
                    if not smp:
                        mm(PB[0:L, 130:131], utincl[0:L, 0:L], g[0:L, 5:6], [utincl, g], [PB])
                        mm(PB[:, 131:132], ones[0:L, :], g[0:L, 5:6], [ones, g], [PB])
                        act(g[0:L, 6:7], PB[0:L, 130:131], AF.Exp, [PB], [g])
                        ts("dve", g[0:L, 7:8], g[0:L, 6:7], -1.0, MUL, [g], [g])
                        act(gb[:, 5:6], PB[:, 131:132], AF.Exp, [PB], [gb])
                        cp("dve", gb[:, 6:7], PB[:, 131:132], [PB], [gb])
                        act(g[0:L, 8:9], PB[0:L, 130:131], AF.Exp, [PB, gb], [g], bias=gb[0:L, 6:7], scale=-1.0)
                        ts("dve", gl[0:L, 0:L], ltstr[0:L, 0:L], g[0:L, 5:6], MUL, [ltstr, g], [gl])
                        mm(PC[0:L, 0:L], gl[0:L, 0:L], utincl[0:L, 0:L], [gl, utincl], [PC])
                        act(GamT[0:L, 0:L], PC[0:L, 0:L], AF.Exp, [PC], [GamT])
                        mm(PB[0:L, 256:256 + L], kT[:, 0:L], kT[:, 0:L], [kT], [PB])
                        mm(PB[0:L, 384:384 + L], kT[:, 0:L], qT[:, 0:L], [kT, qT], [PB])
                        tt("dve", t1[0:L, 0:L], PB[0:L, 256:256 + L], GamT[0:L, 0:L], MUL, [PB, GamT], [t1])
                        stt("dve", Um[0][0:L, 0:L], t1[0:L, 0:L], g[0:L, 1:2], utstr[0:L, 0:L], MUL, MUL, [t1, g, utstr], [Um[0]])
                        tt("dve", t1[0:L, 0:L], PB[0:L, 384:384 + L], GamT[0:L, 0:L], MUL, [PB, GamT], [t1])
                        tt("dve", QKG[0:L, 0:L], t1[0:L, 0:L], utincl[0:L, 0:L], MUL, [t1, utincl], [QKG])
                        idn_i = ident if INV_DT == F32 else identb
                        if INV_DT == F32:
                            tr(PC[0:L, 128:128 + L], Um[0][0:L, 0:L], idn_i[0:L, 0:L], [Um[0], idn_i], [PC])
                            cp("dve", Nm[0][0:L, 0:L], PC[0:L, 128:128 + L], [PC], [Nm[0]])
                        else:
                            tr(PT[0:L, 2, 0:L], Um[0][0:L, 0:L], idn_i[0:L, 0:L], [Um[0], idn_i], [PT])
                            cp("dve", Nm[0][0:L, 0:L], PT[0:L, 2, 0:L], [PT], [Nm[0]])
                        stt("dve", Pm[0:L, 0:L], Um[0][0:L, 0:L], -1.0, ident[0:L, 0:L], MUL, ADD, [Um[0], ident], [Pm])
                        nlev = {128: 6, 16: 3}[L]
                        cur = 0
                        for lev in range(nlev):
                            nx = 1 - cur
                            mm(PC[0:L, 128:128 + L], Um[cur][0:L, 0:L], Nm[cur][0:L, 0:L], [Um[cur], Nm[cur]], [PC])
                            act(Nm[nx][0:L, 0:L], PC[0:L, 128:128 + L], AF.Copy, [PC], [Nm[nx]])
                            if lev < nlev - 1:
                                mm(PC[0:L, 256:256 + L], Nm[cur][0:L, 0:L], Um[cur][0:L, 0:L], [Um[cur], Nm[cur]], [PC])
                                act(Um[nx][0:L, 0:L], PC[0:L, 256:256 + L], AF.Copy, [PC], [Um[nx]])
                            mm(PC[0:L, 384:384 + L], Nm[nx][0:L, 0:L], Pm[0:L, 0:L], [Nm[nx], Pm], [PC])
                            tt("dve", Pm[0:L, 0:L], Pm[0:L, 0:L], PC[0:L, 384:384 + L], ADD, [Pm, PC], [Pm])
                            cur = nx
                        mm(PA[0:L, 0:128], kT[:, 0:L], Sbf[:, 0:128], [kT, Sbf], [PA])
                        stt("dve", Rp[0:L, :], PA[0:L, 0:128], g[0:L, 7:8], vtok[0:L, 0:128], MUL, ADD, [PA, g, vtok], [Rp])
                        mm(PA[0:L, 128:256], Pm[0:L, 0:L], Rp[0:L, :], [Pm, Rp], [PA])
                        ts("dve", vnew[0:L, :], PA[0:L, 128:256], g[0:L, 1:2], MUL, [PA, g], [vnew])
                        mm(PA[0:L, 256:384], qT[:, 0:L], Sbf[:, 0:128], [qT, Sbf], [PA])
                        ts("dve", osb[0:L, :], PA[0:L, 256:384], g[0:L, 6:7], MUL, [PA, g], [osb])
                        mm(PA[0:L, 384:512], QKG[0:L, 0:L], vnew[0:L, :], [QKG, vnew], [PA])
                        tt("dve", osb[0:L, :], osb[0:L, :], PA[0:L, 384:512], ADD, [osb, PA], [osb])
                        ts("dve", kd[0:L, :], ktok[0:L, :], g[0:L, 8:9], MUL, [ktok, g], [kd])
                        mm(PC[:, 0:128], kd[0:L, :], vnew[0:L, :], [kd, vnew], [PC])
                        stt("dve", Sst[:, 0:128], Sst[:, 0:128], gb[:, 5:6], PC[:, 0:128], MUL, ADD, [Sst, gb, PC], [Sst])
                        act(Sbf[:, 0:128], Sst[:, 0:128], AF.Copy, [Sst], [Sbf])
                    else:
                        act(g[0:16, 6:7], g[0:16, 5:6], AF.Exp, [g], [g])
                        ts("dve", g[0:16, 7:8], g[0:16, 6:7], -1.0, MUL, [g], [g])
                        ts("dve", dg16[:, :], ident[0:16, 0:16], g[0:16, 6:7], MUL, [ident, g], [dg16])
                        mm(PC[:, 384:400], ones[0:16, :], dg16[:, :], [ones, dg16], [PC])
                        cp("dve", decb[:, :], PC[:, 384:400], [PC], [decb])
                        cp("dve", qTf[:, :], qT[:, 0:16], [qT], [qTf])
                        cp("dve", kTf[:, :], kT[:, 0:16], [kT], [kTf])
                        tt("dve", QTm[:, :, :], eye16[:, :, :], qTf[:, 0:16].rearrange("p (o s) -> p o s", o=1).to_broadcast([128, 16, 16]), MUL, [eye16, qTf], [QTm])
                        tt("dve", KTm[:, :, :], eye16[:, :, :], kTf[:, 0:16].rearrange("p (o s) -> p o s", o=1).to_broadcast([128, 16, 16]), MUL, [eye16, kTf], [KTm])
                        for i in range(16):
                            mm(PA[0:16, 0:128], KTm[:, i, :], CsG[:, i, :], [KTm, CsB[i]], [PA], start=(i == 0), stop=(i == 15))
                        stt("dve", Rp[0:16, :], PA[0:16, 0:128], g[0:16, 7:8], vtok[0:16, 0:128], MUL, ADD, [PA, g, vtok], [Rp])
                        ts("dve", vnew[0:16, :], Rp[0:16, :], g[0:16, 1:2], MUL, [Rp, g], [vnew])
                        for i in range(16):
                            Kmask = Kmask2[i % 2]
                            PX, pxa = (PC, PC[:, 256:384]) if i % 2 == 0 else (PB, PB[:, 256:384])
                            ts("dve", Kmask[:, :], ktok[0:16, :], ident[0:16, i:i + 1], MUL, [ktok, ident], [Kmask])
                            mm(pxa, Kmask[:, :], vnew[0:16, :], [Kmask, vnew], [PX])
                            stt("dve", CsG[:, i, :], CsG[:, i, :], decb[:, i:i + 1], pxa, MUL, ADD, [CsB[i], decb, PX], [CsB[i]])
                            mm(PA[0:16, 256:384], QTm[:, i, :], CsG[:, i, :], [QTm, CsB[i]], [PA], start=(i == 0), stop=(i == 15))
                        k.dma(oS[l, :, h].rearrange("s d e -> d s e"), CsG[:, :, :], [Cs] + CsB, [oS], Cs)
                        cp("dve", osb[0:16, :], PA[0:16, 256:384], [PA], [osb])
                    act(junk[0:L, 0:128], osb[0:L, :], AF.Square, [osb], [junk, g], accum=g[0:L, 12:13])
                    rstd_from_ss(L, 12, 15, 128.0)
                    stt("dve", ym[0:L, 0:128], osb[0:L, :], g[0:L, 15:16], G[0:L, 0:128], MUL, MUL, [osb, g, G], [ym])
                    out_proj(t, L, 1, wg, [wg[:, l:l + 1]])
                if last_prompt:
                    stv(pS[l, h], Sst[:, 0:128], Sst, pS)

        prompt_tiles = list(range(NT))
        pi = 0
        while pi < len(phases):
            pair = phases[pi:pi + 2]
            if pair[0][1] == "m" and pair[0][2] == 0:
                prologue(pair[0][0])
            streams = []
            for li, (l, kind, h) in enumerate(pair):
                load_weights(lanes[li], l, kind, h)
            for li, (l, kind, h) in enumerate(pair):
                k.rec = []
                run_phase(lanes[li], l, kind, h, prompt_tiles)
                streams.append(k.rec)
                k.rec = None
            k.play(streams)
            if do_samples:
                for li, (l, kind, h) in enumerate(pair):
                    run_phase(lanes[li], l, kind, h, [NT])
            pi += 2
        g = lanes[0].g; junk = xn; mb = lanes[0].mb
        k.dma(CsM[:, 0:4, 0:256], fnw[:, :].rearrange("p (a b) -> p a b", a=4), [fnw], [Cs] + CsB, Cs)
        for t in all_tiles:
            L, tok = tinfo(t)
            act(junk[0:L, :], xt[t][0:L, :], AF.Square, [xt[t]], [junk, g], accum=g[0:L, 0:1])
            act(g[0:L, 1:2], g[0:L, 0:1], AF.Ln, [g], [g], bias=EPS, scale=1.0 / 1024.0)
            act(g[0:L, 1:2], g[0:L, 1:2], AF.Exp, [g], [g], scale=-0.5)
            for a4 in range(4):
                stt("dve", xt[t][0:L, a4 * 256:(a4 + 1) * 256], xt[t][0:L, a4 * 256:(a4 + 1) * 256], g[0:L, 1:2], CsM[0:L, a4, 0:256], MUL, MUL, [xt[t], g, Cs], [xt[t]])
            if t == NT:
                stv(ys[:, :], xt[t][0:16, :], xt[t], ys)
            elif t > 0:
                stv(yp[tok - 16:tok - 16 + L, :], xt[t][0:L, :], xt[t], yp)
        allb = xt + [Cs, pfT, oconv] + [b for ln in lanes for b in (ln.Cext, ln.mb, ln.Sst, ln.xc, ln.g)]
        k.final_wait("sp", allb)
        print('total ops', k.nops)
        k.emit()
    return nc


def _consts():
    i = np.arange(128)
    c = {}
    c["cident"] = np.eye(128, dtype=np.float32)
    c["cutincl"] = (i[:, None] <= i[None, :]).astype(np.float32)
    c["cutstr"] = (i[:, None] < i[None, :]).astype(np.float32)
    c["cltstr"] = (i[:, None] > i[None, :]).astype(np.float32)
    c["ceye16"] = np.ascontiguousarray(np.broadcast_to(np.eye(16, dtype=np.float32)[None], (128, 16, 16)))
    return c


def _pk(w):
    C = w.shape[1]
    return w.reshape(8, 128, C).transpose(1, 0, 2)


_NC_CACHE = {}


def kernel(x_prompt, x_sample, state_mlstm_C, state_mlstm_n, state_mlstm_m, state_gdn_S, state_gdn_conv,
           meta_tokens, norm_w, w_in, b_igate, b_fgate, w_mlstm_norm, conv_w, a_log, dt_bias,
           w_gdn_norm, w_out, final_norm_w, _n_layers=DEPTH, _do_samples=True, _max_phases=None):
    f = lambda a: np.ascontiguousarray(np.asarray(a, dtype=np.float32))
    x_prompt, x_sample, meta_tokens, w_in, w_out = f(x_prompt), f(x_sample), f(meta_tokens), f(w_in), f(w_out)
    OFF = dict(qm=0, km=512, vm=1024, om=2048, zm=3072, im=4096, fm=4100, qg=4104, kg=5128, vg=6152, zg=7176, bg=8200, ag=8208)
    WM = np.empty((DEPTH, 4, 128, 8, 1026), np.float32)
    WG = np.empty((DEPTH, 8, 128, 8, 514), np.float32)
    for l in range(DEPTH):
        w = w_in[l]
        for h in range(4):
            cols = np.concatenate([np.arange(OFF["qm"] + h * 128, OFF["qm"] + (h + 1) * 128),
                                   np.arange(OFF["km"] + h * 128, OFF["km"] + (h + 1) * 128),
                                   np.arange(OFF["vm"] + h * 256, OFF["vm"] + (h + 1) * 256),
                                   np.arange(OFF["om"] + h * 256, OFF["om"] + (h + 1) * 256),
                                   np.arange(OFF["zm"] + h * 256, OFF["zm"] + (h + 1) * 256),
                                   [OFF["im"] + h, OFF["fm"] + h]]).astype(np.int64)
            WM[l, h] = _pk(w[:, cols])
        for h in range(8):
            cols = np.concatenate([np.arange(OFF[kk] + h * 128, OFF[kk] + (h + 1) * 128) for kk in ("qg", "kg", "vg", "zg")]
                                  + [[OFF["bg"] + h, OFF["ag"] + h]]).astype(np.int64)
            WG[l, h] = _pk(w[:, cols])
    WO = np.ascontiguousarray(w_out.reshape(DEPTH, 16, 128, 1024))
    nwT = np.ascontiguousarray(f(norm_w).reshape(DEPTH, 8, 128).transpose(2, 0, 1).reshape(128, DEPTH * 8))
    wmT = np.ascontiguousarray(f(w_mlstm_norm).reshape(DEPTH, 8, 128).transpose(2, 0, 1).reshape(128, DEPTH * 8))
    wgT = np.ascontiguousarray(f(w_gdn_norm).T)
    cwT = np.ascontiguousarray(f(conv_w).reshape(DEPTH, 4, 3, 8, 128).transpose(4, 0, 2, 3, 1).reshape(128, DEPTH * 96))
    gpr = np.concatenate([f(b_igate), f(b_fgate), f(a_log), f(dt_bias)], axis=1).reshape(1, DEPTH * 24)
    gpb = np.ascontiguousarray(np.broadcast_to(gpr, (128, DEPTH * 24)))
    fnw = np.ascontiguousarray(np.broadcast_to(f(final_norm_w)[None], (128, 1024)))
    consts = _consts()
    key = (_n_layers, _do_samples, _max_phases)
    if key not in _NC_CACHE:
        _NC_CACHE[key] = build_program(_n_layers, _do_samples, _max_phases)
    nc = _NC_CACHE[key]
    sC_, sn_, sm_, sS_, sv_ = f(state_mlstm_C), f(state_mlstm_n), f(state_mlstm_m), f(state_gdn_S), f(state_gdn_conv)
    in_maps = []
    for c in range(8):
        sl = slice(16 * c, 16 * c + 16)
        m = dict(xp=np.concatenate([meta_tokens, x_prompt[c]], axis=0), xsm=np.ascontiguousarray(x_sample[sl, 0, :]),
                 WM=WM, WG=WG, WO=WO,
                 sC=np.ascontiguousarray(sC_[:, sl]), sn=np.ascontiguousarray(sn_[:, sl]), sm=np.ascontiguousarray(sm_[:, sl]),
                 sS=np.ascontiguousarray(sS_[:, sl]), sconv=np.ascontiguousarray(sv_[:, sl]),
                 nwT=nwT, wmT=wmT, wgT=wgT, cwT=cwT, gpb=gpb, fnw=fnw)
        m.update(consts)
        in_maps.append(m)
    res = run_bass_kernel_spmd(nc, in_maps, core_ids=list(range(8)))
    R = res.results
    cat = lambda name, ax: np.concatenate([np.expand_dims(r[name], ax) if False else r[name] for r in R], axis=ax)
    y_prompt = np.stack([r["yp"] for r in R], 0)
    y_sample = np.concatenate([r["ys"] for r in R], 0)[:, None, :]
    p_C = np.stack([r["pC"] for r in R], 1); p_n = np.stack([r["pn"] for r in R], 1); p_m = np.stack([r["pm"] for r in R], 1)
    p_S = np.stack([r["pS"] for r in R], 1); p_conv = np.stack([r["pconv"] for r in R], 1)
    s_C = np.concatenate([r["oC"] for r in R], 1); s_n = np.concatenate([r["on"] for r in R], 1)
    s_m = np.concatenate([r["om"] for r in R], 1); s_S = np.concatenate([r["oS"] for r in R], 1)
    s_conv = np.concatenate([r["oconv"] for r in R], 1)
    outs = (y_prompt, y_sample, p_C, p_n, p_m, p_S, p_conv, s_C, s_n, s_m, s_S, s_conv)
    return tuple(np.ascontiguousarray(o.astype(np.float32)) for o in outs)
```

```python
from contextlib import ExitStack
import numpy as np
import concourse.bass as bass
import concourse.mybir as mybir
from concourse.bass_utils import run_bass_kernel_spmd

F32 = mybir.dt.float32
BF16 = mybir.dt.bfloat16
ALU = mybir.AluOpType
AF = mybir.ActivationFunctionType
AX = mybir.AxisListType

ENGS = ("pe", "act", "dve", "pool", "sp")
SEM_EPOCH = 30000


class Buf:
    __slots__ = ("name", "t", "lw", "rd", "dsem", "dcnt", "psum")

    def __init__(self, name, t):
        self.name = name
        self.t = t
        self.lw = None
        self.rd = []
        self.dsem = None
        self.dcnt = 0
        self.psum = False

    def __getitem__(self, idx):
        return self.t[idx]


class KB:
    def __init__(self, nc, stack):
        self.nc = nc
        self.stack = stack
        self.ops = {e: [] for e in ENGS}
        self.cnt = {e: 0 for e in ENGS}
        self.esems = {e: [] for e in ENGS}
        self.waited = {e: {} for e in ENGS}
        self.nbuf = 0
        self.maxops = None
        self.rec = None
        self.nops = 0
        self.final = False

    def sb(self, name, shape, dt=F32):
        t = self.stack.enter_context(self.nc.sbuf_tensor(name, list(shape), dt))
        return Buf(name, t)

    def ps(self, name, shape, dt=F32):
        t = self.stack.enter_context(self.nc.psum_tensor(name, list(shape), dt))
        b = Buf(name, t)
        b.psum = True
        return b

    def dram(self, name, shape, dt=F32, kind="ExternalInput"):
        t = self.nc.dram_tensor(name, list(shape), dt, kind=kind)
        return Buf(name, t.ap())

    def _esem(self, eng, idx):
        ep = (idx - 1) // SEM_EPOCH
        lst = self.esems[eng]
        while len(lst) <= ep:
            lst.append(self.stack.enter_context(self.nc.semaphore(f"s_{eng}_{len(lst)}")))
        return lst[ep], idx - ep * SEM_EPOCH

    def _deps(self, reads, writes):
        deps = []
        for b in list(reads) + list(writes):
            if b.lw is not None:
                deps.append(b.lw)
        for b in writes:
            deps.extend(b.rd)
        for b in reads:
            if b.psum:
                deps.extend(b.rd)
        return deps

    def _mark(self, tok, reads, writes):
        for b in reads:
            b.rd.append(tok)
        for b in writes:
            b.lw = tok
            b.rd = []

    def _waits(self, eng, deps):
        wd = self.waited[eng]
        best = {}
        for d in deps:
            if d[0] == "e":
                _, e2, i2 = d
                if e2 == eng and eng in ("pe", "sp"):
                    continue
                key = ("e", e2)
                if i2 > best.get(key, (0, None))[0]:
                    best[key] = (i2, None)
            else:
                _, b, v = d
                v = b.dcnt
                key = ("d", b.name)
                if v > best.get(key, (0, None))[0]:
                    best[key] = (v, b)
        waits = []
        for key, (v, b) in best.items():
            if wd.get(key, 0) >= v:
                continue
            wd[key] = v
            if key[0] == "e":
                waits.append(self._esem(key[1], v))
            else:
                waits.append((b.dsem, v))
        return waits

    def op(self, eng, fn, reads=(), writes=()):
        if self.rec is not None:
            self.rec.append(("op", (eng, fn, tuple(reads), tuple(writes)), {}))
            return
        self.nops += 1
        if self.maxops is not None and self.nops > self.maxops:
            return
        deps = self._deps(reads, writes)
        waits = self._waits(eng, deps)
        self.cnt[eng] += 1
        idx = self.cnt[eng]
        self.ops[eng].append((waits, fn, (self._esem(eng, idx)[0], 1)))
        self._mark(("e", eng, idx), reads, writes)

    def dma(self, out_ap, in_ap, reads, writes, sembuf, eng="sp", **kw):
        if self.rec is not None:
            self.rec.append(("dma", (out_ap, in_ap, tuple(reads), tuple(writes), sembuf), dict(eng=eng, **kw)))
            return
        self.nops += 1
        if self.maxops is not None and self.nops > self.maxops:
            return
        deps = self._deps(reads, writes)
        waits = self._waits(eng, deps)
        if sembuf.dsem is None:
            sembuf.dsem = self.stack.enter_context(self.nc.semaphore(f"d_{sembuf.name}"))
        sembuf.dcnt += 16
        v = sembuf.dcnt
        sem = sembuf.dsem

        def fn(e):
            return e.dma_start(out=out_ap, in_=in_ap, **kw)
        self.ops[eng].append((waits, fn, (sem, 16)))
        self._mark(("d", sembuf, v), reads, writes)

    def play(self, streams):
        idx = [0] * len(streams)
        live = True
        while live:
            live = False
            for si, st_ in enumerate(streams):
                if idx[si] < len(st_):
                    kind, a, kw = st_[idx[si]]
                    idx[si] += 1
                    live = True
                    if kind == "op":
                        self.op(*a)
                    else:
                        self.dma(*a, **kw)

    def final_wait(self, eng, bufs):
        waits = []
        for b in bufs:
            if b.dsem is not None:
                waits.append((b.dsem, b.dcnt))
        self.ops[eng].append((waits, None, None))

    def emit(self):
        nc = self.nc
        with nc.Block() as block:
            def run(e, lst, is_dma_ok=True):
                for waits, fn, inc in lst:
                    for (s, v) in waits:
                        e.wait_ge(s, v)
                    if fn is None:
                        continue
                    ins = fn(e)
                    if inc is not None:
                        ins.then_inc(inc[0], inc[1])

            @block.tensor
            def _(e):
                run(e, self.ops["pe"])

            @block.scalar
            def _(e):
                run(e, self.ops["act"])

            @block.vector
            def _(e):
                run(e, self.ops["dve"])

            @block.gpsimd
            def _(e):
                run(e, self.ops["pool"])

            @block.sync
            def _(e):
                run(e, self.ops["sp"])


DEPTH = 4
NT = 17
TOK0 = [0] + [16 + 128 * j for j in range(16)]
TLEN = [16] + [128] * 16
STOK = 2064
EPS = 1e-6
NLAYERS_BUILD = DEPTH
INV_DT = F32


def build_program(n_layers=DEPTH, do_samples=True, max_phases=None):
    nc = bass.Bass("TRN2", target_bir_lowering=False)
    st = ExitStack()
    with st:
        k = KB(nc, st)
        import os
        if os.environ.get('KMAXOPS'):
            k.maxops = int(os.environ['KMAXOPS'])
        D = {}
        def din(name, shape):
            D[name] = k.dram(name, shape)
            return D[name]
        def dout(name, shape):
            D[name] = k.dram(name, shape, kind="ExternalOutput")
            return D[name]
        xp = din("xp", [2064, 1024]); xsm = din("xsm", [16, 1024])
        WM = din("WM", [DEPTH, 4, 128, 8, 1026]); WG = din("WG", [DEPTH, 8, 128, 8, 514])
        WO = din("WO", [DEPTH, 16, 128, 1024])
        sC = din("sC", [DEPTH, 16, 4, 128, 256]); sn = din("sn", [DEPTH, 16, 4, 128]); sm = din("sm", [DEPTH, 16, 4])
        sS = din("sS", [DEPTH, 16, 8, 128, 128]); sconv = din("sconv", [DEPTH, 16, 3, 3072])
        cident = din("cident", [128, 128]); cutincl = din("cutincl", [128, 128]); cutstr = din("cutstr", [128, 128])
        cltstr = din("cltstr", [128, 128]); ceye16 = din("ceye16", [128, 16, 16])
        nwT = din("nwT", [128, DEPTH * 8]); wmT = din("wmT", [128, DEPTH * 8]); wgT = din("wgT", [128, DEPTH])
        cwT = din("cwT", [128, DEPTH * 24 * 4]); gpb = din("gpb", [128, DEPTH * 24]); fnw = din("fnw", [128, 1024])
        yp = dout("yp", [2048, 1024]); ys = dout("ys", [16, 1024])
        pC = dout("pC", [DEPTH, 4, 128, 256]); pn = dout("pn", [DEPTH, 4, 128]); pm = dout("pm", [DEPTH, 4])
        pS = dout("pS", [DEPTH, 8, 128, 128]); pconv = dout("pconv", [DEPTH, 3, 3072])
        oC = dout("oC", [DEPTH, 16, 4, 128, 256]); on = dout("on", [DEPTH, 16, 4, 128]); om = dout("om", [DEPTH, 16, 4])
        oS = dout("oS", [DEPTH, 16, 8, 128, 128]); oconv = dout("oconv", [DEPTH, 16, 3, 3072])

        class NS:
            pass
        xt = [k.sb(f"x{t}", [128, 1024]) for t in range(NT + 1)]
        hT_t = st.enter_context(nc.sbuf_tensor("hT", [128, 8, 2080], BF16))
        hT = [Buf(f"hT{t}", hT_t) for t in range(NT + 1)]
        ident = k.sb("ident", [128, 128]); utincl = k.sb("utincl", [128, 128]); utstr = k.sb("utstr", [128, 128])
        ltstr = k.sb("ltstr", [128, 128]); ones = k.sb("ones", [128, 128]); eye16 = k.sb("eye16", [128, 16, 16])
        identb = k.sb("identb", [128, 128], BF16); onesb = k.sb("onesb", [128, 128], BF16)
        nw = k.sb("nw", [128, DEPTH * 8]); wm = k.sb("wm", [128, DEPTH * 8]); wg = k.sb("wg", [128, DEPTH])
        cw = k.sb("cw", [128, DEPTH * 96]); gp = k.sb("gp", [128, DEPTH * 24]); gpd = k.sb("gpd", [128, DEPTH * 24])
        xn = k.sb("xn", [128, 1024], BF16)
        LANE_NAMES = ("Wb Wob g gb mb junk ctmp dg ktok Vp qT qTd kT QKm E G ym yT Cext Cbf Sst Sbf xc cs cacc sq rs vTb vtok "
                      "gl GamT t1 Um Nm Pm Rp QKG vnew kd osb PA PB PC PT").split()
        def mk_lane(i):
            ln = NS()
            sb = lambda name, shape, dt=F32: k.sb(f"{name}_{i}", shape, dt)
            ln.Wb = sb("Wb", [128, 8, 1026], BF16); ln.Wob = sb("Wob", [128, 2, 1024], BF16)
            ln.g = sb("g", [128, 40]); ln.gb = sb("gb", [128, 16]); ln.mb = sb("mb", [128, 1])
            ln.junk = sb("junk", [128, 256], BF16); ln.ctmp = sb("ctmp", [128, 9]); ln.dg = sb("dg", [128, 128])
            ln.ktok = sb("ktok", [128, 128], BF16); ln.Vp = sb("Vp", [128, 258], BF16)
            ln.qT = sb("qT", [128, 128], BF16); ln.qTd = sb("qTd", [128, 128], BF16); ln.kT = sb("kT", [128, 128], BF16)
            ln.QKm = sb("QKm", [128, 128], BF16); ln.E = sb("E", [128, 512]); ln.G = sb("G", [128, 256])
            ln.ym = sb("ym", [128, 256], BF16); ln.yT = sb("yT", [128, 2, 128], BF16)
            ln.Cext = sb("Cext", [128, 257]); ln.Cbf = sb("Cbf", [128, 257], BF16)
            ln.Sst = ln.Cext; ln.Sbf = ln.Cbf
            ln.xc = sb("xc", [128, 3, 131]); ln.cs = sb("cs", [128, 3, 128]); ln.cacc = sb("cacc", [128, 3, 128])
            ln.sq = sb("sq", [128, 256], BF16); ln.rs = sb("rs", [128, 256])
            ln.vTb = sb("vTb", [128, 128], BF16); ln.vtok = ln.Vp
            ln.gl = sb("gl", [128, 128]); ln.GamT = sb("GamT", [128, 128]); ln.t1 = ln.dg
            ln.Um = [sb(f"Um{j}", [128, 128], INV_DT) for j in range(2)]
            ln.Nm = [sb(f"Nm{j}", [128, 128], INV_DT) for j in range(2)]
            ln.Pm = sb("Pm", [128, 128], INV_DT); ln.Rp = sb("Rp", [128, 128], INV_DT)
            ln.QKG = ln.QKm; ln.vnew = sb("vnew", [128, 128], BF16); ln.kd = ln.qTd
            ln.osb = sb("osb", [128, 128])
            ln.PA = k.ps(f"PA_{i}", [128, 512]); ln.PB = k.ps(f"PB_{i}", [128, 512]); ln.PC = k.ps(f"PC_{i}", [128, 512])
            ln.PT = k.ps(f"PT_{i}", [128, 8, 128], BF16)
            return ln
        lanes = [mk_lane(0), mk_lane(1)]
        Cs = k.sb("Cs", [128, 2056])
        CsM = Cs[:, :].rearrange("p (s e) -> p s e", e=257)
        CsG = Cs[:, 0:2048].rearrange("p (s e) -> p s e", e=128)
        CsB = [Buf(f"Cs_s{j}", Cs.t) for j in range(16)]
        msm = k.sb("msm", [16, 4]); hist = k.sb("hist", [16, 3, 128]); histT = k.sb("histT", [128, 9, 16])
        qTf = k.sb("qTf", [128, 16]); kTf = k.sb("kTf", [128, 16])
        QTm = k.sb("QTm", [128, 16, 16]); KTm = k.sb("KTm", [128, 16, 16])
        Kp = k.sb("Kp", [16, 128], BF16); Kmask2 = [k.sb(f"Kmask{j}", [16, 128], BF16) for j in range(2)]; Vx = k.sb("Vx", [16, 258], BF16)
        decb = k.sb("decb", [128, 16]); dg16 = k.sb("dg16", [16, 16])
        pfT = k.sb("pfT", [128, 3, 16])

        def mm(out, lhsT, rhs, R, W, start=True, stop=True):
            k.op("pe", lambda e: e.matmul(out=out, lhsT=lhsT, rhs=rhs, start=start, stop=stop), R, W)
        def tr(out, in_, idn, R, W):
            k.op("pe", lambda e: e.transpose(out=out, in_=in_, identity=idn), R, W)
        def act(out, in_, func, R, W, bias=0.0, scale=1.0, accum=None, eng="act"):
            if accum is None:
                k.op("act", lambda e: e.activation(out=out, in_=in_, func=func, bias=bias, scale=scale), R, W)
            else:
                k.op("act", lambda e: e.activation(out=out, in_=in_, func=func, bias=bias, scale=scale, accum_out=accum), R, W)
        def tt(eng, out, in0, in1, op, R, W):
            k.op(eng, lambda e: e.tensor_tensor(out=out, in0=in0, in1=in1, op=op), R, W)
        def ts(eng, out, in0, s1, op0, R, W, s2=None, op1=None):
            if op1 is None:
                k.op(eng, lambda e: e.tensor_scalar(out=out, in0=in0, scalar1=s1, scalar2=None, op0=op0), R, W)
            else:
                k.op(eng, lambda e: e.tensor_scalar(out=out, in0=in0, scalar1=s1, scalar2=s2, op0=op0, op1=op1), R, W)
        def stt(eng, out, in0, sc, in1, op0, op1, R, W):
            k.op(eng, lambda e: e.scalar_tensor_tensor(out=out, in0=in0, scalar=sc, in1=in1, op0=op0, op1=op1), R, W)
        def cp(eng, out, in_, R, W):
            k.op(eng, lambda e: e.tensor_copy(out=out, in_=in_), R, W)
        def recip(out, in_, R, W):
            k.op("dve", lambda e: e.reciprocal(out=out, in_=in_), R, W)
        def ld(out, in_, dbuf, sbuf, eng="sp", **kw):
            k.dma(out, in_, [dbuf], [sbuf], sbuf, eng=eng, **kw)
        def stv(out, in_, sbuf, dbuf, **kw):
            k.dma(out, in_, [sbuf], [dbuf], sbuf, **kw)
        MUL, ADD, SUB, MAX = ALU.mult, ALU.add, ALU.subtract, ALU.max

        ld(ident[:], cident[:], cident, ident); ld(utincl[:], cutincl[:], cutincl, utincl)
        ld(utstr[:], cutstr[:], cutstr, utstr); ld(ltstr[:], cltstr[:], cltstr, ltstr)
        ld(eye16[:], ceye16[:], ceye16, eye16)
        ld(nw[:], nwT[:], nwT, nw); ld(wm[:], wmT[:], wmT, wm); ld(wg[:], wgT[:], wgT, wg)
        ld(cw[:], cwT[:], cwT, cw); ld(gp[:], gpb[:], gpb, gp)
        k.op("pool", lambda e: e.memset(ones[:], 1.0), [], [ones])
        k.op("pool", lambda e: e.memset(onesb[:], 1.0), [], [onesb])
        cp("dve", identb[:], ident[:], [ident], [identb])
        cp("dve", gpd[:], gp[:], [gp], [gpd])
        for l in range(DEPTH):
            ts("dve", gpd[:, l * 24 + 4:l * 24 + 8], gp[:, l * 24 + 4:l * 24 + 8], -1.0, MUL, [gp], [gpd])
            act(gpd[:, l * 24 + 8:l * 24 + 16], gp[:, l * 24 + 8:l * 24 + 16], AF.Exp, [gp], [gpd])
            ts("dve", gpd[:, l * 24 + 8:l * 24 + 16], gpd[:, l * 24 + 8:l * 24 + 16], -1.0, MUL, [gpd], [gpd])
        for t in range(NT):
            L = TLEN[t]
            ld(xt[t][0:L, :], xp[TOK0[t]:TOK0[t] + L, :], xp, xt[t])
        ld(xt[NT][0:16, :], xsm[:, :], xsm, xt[NT])

        all_tiles = list(range(NT)) + ([NT] if do_samples else [])
        def tinfo(t):
            if t == NT:
                return 16, STOK
            return TLEN[t], TOK0[t]

        def load_weights(ln, l, kind, h):
            xb = ([Cs, msm, hist, pfT, ln.g] + CsB) if do_samples else []
            if kind == "m":
                k.dma(ln.Wb[:, :, 0:1026], WM[l, h], [WM], [ln.Wb] + xb, ln.Wb, eng="pool")
                k.dma(ln.Wob[:, 0:2, :], WO[l, 2 * h:2 * h + 2].rearrange("j p c -> p j c"), [WO], [ln.Wob], ln.Wob, eng="pool")
            else:
                k.dma(ln.Wb[:, :, 0:514], WG[l, h], [WG], [ln.Wb] + xb, ln.Wb, eng="pool")
                k.dma(ln.Wob[:, 0:1, :], WO[l, 8 + h:9 + h].rearrange("j p c -> p j c"), [WO], [ln.Wob], ln.Wob, eng="pool")

        phases = [(l, kind, h) for l in range(n_layers) for (kind, nh) in (("m", 4), ("g", 8)) for h in range(nh)]
        if max_phases is not None:
            phases = phases[:max_phases]

        def prologue(l):
            ln = lanes[0]
            g = ln.g; T0 = ln.PT; junk = xn
            for t in all_tiles:
                L, tok = tinfo(t)
                act(xn[0:L, :], xt[t][0:L, :], AF.Square, [xt[t]], [xn, g], accum=g[0:L, 0:1])
                act(g[0:L, 1:2], g[0:L, 0:1], AF.Ln, [g], [g], bias=EPS, scale=1.0 / 1024.0)
                act(g[0:L, 1:2], g[0:L, 1:2], AF.Exp, [g], [g], scale=-0.5)
                ts("dve", xn[0:L, :], xt[t][0:L, :], g[0:L, 1:2], MUL, [xt[t], g], [xn])
                for kt in range(8):
                    tr(T0[:, kt, 0:L], xn[0:L, kt * 128:(kt + 1) * 128], identb[0:L, 0:L], [xn, identb], [T0])
                tt("dve", hT_t[:, :, tok:tok + L], T0[:, :, 0:L],
                   nw[:, l * 8:(l + 1) * 8].rearrange("p (k o) -> p k o", o=1).to_broadcast([128, 8, L]),
                   MUL, [T0, nw], [hT[t]])

        def run_phase(ln, l, kind, h, tiles):
            (Wb_, Wob_, g, gb, mb, junk, ctmp, dg, ktok, Vp, qT, qTd, kT, QKm, E, G, ym, yT, Cext, Cbf, Sst, Sbf, xc, cs, cacc,
             sq, rs, vTb, vtok, gl, GamT, t1, Um, Nm, Pm, Rp, QKG, vnew, kd, osb, PA, PB, PC, PT) = [getattr(ln, n) for n in LANE_NAMES]
            W = Wb_; Wo = Wob_
            gq = l * 24
            sck = 128.0 ** -0.5

            def rstd_from_ss(L, col_ss, col_out, n):
                act(g[0:L, col_out:col_out + 1], g[0:L, col_ss:col_ss + 1], AF.Ln, [g], [g], bias=EPS, scale=1.0 / n)
                act(g[0:L, col_out:col_out + 1], g[0:L, col_out:col_out + 1], AF.Exp, [g], [g], scale=-0.5)

            def out_proj(t, L, nj, wbuf, wcols):
                for j in range(nj):
                    tr(PT[:, 4 + j, 0:L], ym[0:L, j * 128:(j + 1) * 128], identb[0:L, 0:L], [ym, identb], [PT])
                for j in range(nj):
                    ts("dve", yT[:, j, 0:L], PT[:, 4 + j, 0:L], wcols[j], MUL, [PT, wbuf], [yT])
                for nb, PX in ((0, PA), (1, PB)):
                    for j in range(nj):
                        mm(PX[0:L, 0:512], yT[:, j, 0:L], Wo[:, j, nb * 512:(nb + 1) * 512],
                           [yT, Wo], [PX], start=(j == 0), stop=(j == nj - 1))
                    tt("dve", xt[t][0:L, nb * 512:(nb + 1) * 512], xt[t][0:L, nb * 512:(nb + 1) * 512], PX[0:L, 0:512], ADD, [xt[t], PX], [xt[t]])

            first = (tiles[0] == 0)
            last_prompt = (NT - 1 in tiles)
            if kind == "m":
                if first:
                    k.op("pool", lambda e: e.memset(Cext[:], 0.0), [], [Cext])
                    k.op("pool", lambda e: e.memset(Cbf[:], 0.0), [], [Cbf])
                    k.op("pool", lambda e: e.memset(mb[:], 0.0), [], [mb])
                for t in tiles:
                    L, tok = tinfo(t)
                    smp = (t == NT)
                    if smp:
                        ld(msm[:, :], sm[l, :, :], sm, msm)
                    for kt in range(8):
                        mm(PA[0:L, 0:384], hT_t[:, kt, tok:tok + L], W[:, kt, 128:512], [hT[t], W], [PA], start=(kt == 0), stop=(kt == 7))
                    for kt in range(8):
                        mm(PA[0:L, 384:386], hT_t[:, kt, tok:tok + L], W[:, kt, 1024:1026], [hT[t], W], [PA], start=(kt == 0), stop=(kt == 7))
                    for kt in range(8):
                        mm(PB[0:L, 0:512], hT_t[:, kt, tok:tok + L], W[:, kt, 512:1024], [hT[t], W], [PB], start=(kt == 0), stop=(kt == 7))
                    for j in range(2):
                        for kt in range(8):
                            mm(PC[:, j * L:(j + 1) * L], W[:, kt, j * 128:(j + 1) * 128], hT_t[:, kt, tok:tok + L], [hT[t], W], [PC], start=(kt == 0), stop=(kt == 7))
                    act(E[0:L, :], PB[0:L, 0:512], AF.Exp, [PB], [E], scale=-1.0)
                    ts("dve", E[0:L, :], E[0:L, :], 1.0, ADD, [E], [E])
                    tt("dve", E[0:L, 0:256], E[0:L, 0:256], E[0:L, 256:512], MUL, [E], [E])
                    recip(E[0:L, 0:256], E[0:L, 0:256], [E], [E])
                    tt("dve", G[0:L, :], E[0:L, 0:256], PB[0:L, 256:512], MUL, [E, PB], [G])
                    act(g[0:L, 0:2], PA[0:L, 384:386], AF.Copy, [PA], [g])
                    ts("dve", g[0:L, 2:3], g[0:L, 0:1], gpd[0:L, gq + h:gq + h + 1], ADD, [g, gpd], [g])
                    act(g[0:L, 3:4], g[0:L, 1:2], AF.Exp, [g, gpd], [g], bias=gpd[0:L, gq + 4 + h:gq + 5 + h], scale=-1.0)
                    act(g[0:L, 4:5], g[0:L, 3:4], AF.Ln, [g], [g], bias=1.0)
                    ts("dve", g[0:L, 5:6], g[0:L, 4:5], -1.0, MUL, [g], [g])
                    if not smp:
                        mm(PC[0:L, 384:385], utincl[0:L, 0:L], g[0:L, 5:6], [utincl, g], [PC])
                        mm(PC[:, 385:386], ones[0:L, :], g[0:L, 5:6], [ones, g], [PC])
                        tt("dve", g[0:L, 6:7], g[0:L, 2:3], PC[0:L, 384:385], SUB, [g, PC], [g])
                        ts("dve", dg[0:L, 0:L], ident[0:L, 0:L], g[0:L, 6:7], MUL, [ident, g], [dg])
                        mm(PC[:, 256:256 + L], ones[0:L, :], dg[0:L, 0:L], [ones, dg], [PC])
                        k.op("dve", lambda e, L=L: e.tensor_reduce(out=gb[:, 0:1], in_=PC[:, 256:256 + L], axis=AX.X, op=MAX), [PC], [gb])
                        tt("dve", gb[:, 1:2], gb[:, 0:1], mb[:, 0:1], MAX, [gb, mb], [gb])
                        ts("dve", gb[:, 2:3], gb[:, 1:2], -1.0, MUL, [gb], [gb])
                        act(gb[:, 3:4], mb[:, 0:1], AF.Exp, [mb, gb], [gb], bias=gb[:, 2:3])
                        act(g[0:L, 7:8], g[0:L, 6:7], AF.Exp, [g, gb], [g], bias=gb[0:L, 2:3])
                        tt("dve", gb[:, 4:5], PC[:, 385:386], gb[:, 1:2], ADD, [PC, gb], [gb])
                        act(g[0:L, 8:9], PC[0:L, 384:385], AF.Exp, [PC, gb], [g], bias=gb[0:L, 2:3], scale=-1.0)
                    else:
                        tt("dve", g[0:L, 6:7], g[0:L, 5:6], msm[0:16, h:h + 1], ADD, [g, msm], [g])
                        tt("dve", g[0:L, 20:21], g[0:L, 6:7], g[0:L, 2:3], MAX, [g], [g])
                        ts("dve", g[0:L, 21:22], g[0:L, 20:21], -1.0, MUL, [g], [g])
                        act(g[0:L, 22:23], g[0:L, 6:7], AF.Exp, [g], [g], bias=g[0:L, 21:22])
                        act(g[0:L, 7:8], g[0:L, 2:3], AF.Exp, [g], [g], bias=g[0:L, 21:22])
                        act(g[0:L, 8:9], g[0:L, 20:21], AF.Exp, [g], [g], scale=-1.0)
                        stv(om[l, :, h:h + 1], g[0:16, 20:21], g, om, allow_slow_non_contiguous=True)
                    act(qT[:, 0:L], PC[:, 0:L], AF.Copy, [PC], [qT])
                    ts("dve", kT[:, 0:L], PC[:, L:2 * L], sck, MUL, [PC], [kT])
                    if not smp:
                        ts("dve", ktok[0:L, :], PA[0:L, 0:128], sck, MUL, [PA], [ktok])
                        ts("dve", Vp[0:L, 0:256], PA[0:L, 128:384], g[0:L, 7:8], MUL, [PA, g], [Vp])
                        cp("dve", Vp[0:L, 256:258], g[0:L, 7:8].to_broadcast([L, 2]), [g], [Vp])
                        ts("dve", qTd[:, 0:L], PC[:, 0:L], gb[:, 3:4], MUL, [PC, gb], [qTd])
                        mm(PC[0:L, 256:256 + L], kT[:, 0:L], qT[:, 0:L], [kT, qT], [PC])
                        tt("dve", QKm[0:L, 0:L], PC[0:L, 256:256 + L], utincl[0:L, 0:L], MUL, [PC, utincl], [QKm])
                        mm(PA[0:L, 0:257], qTd[:, 0:L], Cbf[:, :], [qTd, Cbf], [PA], start=True, stop=False)
                        mm(PA[0:L, 0:257], QKm[0:L, 0:L], Vp[0:L, 0:257], [QKm, Vp], [PA], start=False, stop=True)
                    else:
                        ts("dve", Kp[0:16, :], PA[0:16, 0:128], g[0:16, 7:8], MUL, [PA, g], [Kp], s2=sck, op1=MUL)
                        cp("dve", Vx[0:16, 0:256], PA[0:16, 128:384], [PA], [Vx])
                        k.op("pool", lambda e: e.memset(Vx[0:16, 256:258], 1.0), [], [Vx])
                        ts("dve", dg16[:, :], ident[0:16, 0:16], g[0:16, 22:23], MUL, [ident, g], [dg16])
                        mm(PC[:, 256:272], ones[0:16, :], dg16[:, :], [ones, dg16], [PC])
                        cp("dve", decb[:, :], PC[:, 256:272], [PC], [decb])
                        cp("dve", qTf[:, :], PC[:, 0:16], [PC], [qTf])
                        tt("dve", QTm[:, :, :], eye16[:, :, :], qTf[:, 0:16].rearrange("p (o s) -> p o s", o=1).to_broadcast([128, 16, 16]), MUL, [eye16, qTf], [QTm])
                        for hf in range(2):
                            s0 = 8 * hf
                            k.dma(CsM[:, :, 0:256], sC[l, s0:s0 + 8, h].rearrange("s d e -> d s e"), [sC], [Cs] + CsB[0:8], Cs)
                            k.dma(CsM[:, :, 256:257], sn[l, s0:s0 + 8, h, :].rearrange("s (d o) -> d s o", o=1), [sn], [Cs] + CsB[0:8], Cs, allow_slow_non_contiguous=True)
                            for i8 in range(8):
                                i = s0 + i8
                                Kmask = Kmask2[i % 2]
                                PX, pxa = (PB, PB[:, 0:257]) if i % 2 == 0 else (PC, PC[:, 0:257])
                                ts("dve", Kmask[:, :], Kp[:, :], ident[0:16, i:i + 1], MUL, [Kp, ident], [Kmask])
                                mm(pxa, Kmask[:, :], Vx[:, 0:257], [Kmask, Vx], [PX])
                                stt("dve", CsM[:, i8, :], CsM[:, i8, :], decb[:, i:i + 1], pxa, MUL, ADD, [CsB[i8], decb, PX], [CsB[i8]])
                                mm(PA[0:16, 0:257], QTm[:, i, :], CsM[:, i8, :], [QTm, CsB[i8]], [PA], start=(i == 0), stop=(i == 15))
                            k.dma(oC[l, s0:s0 + 8, h].rearrange("s d e -> d s e"), CsM[:, :, 0:256], [Cs] + CsB[0:8], [oC], Cs)
                            k.dma(on[l, s0:s0 + 8, h, :].rearrange("s (d o) -> d s o", o=1), CsM[:, :, 256:257], [Cs] + CsB[0:8], [on], Cs, allow_slow_non_contiguous=True)
                    ts("dve", g[0:L, 9:10], PA[0:L, 256:257], -1.0, MUL, [PA], [g])
                    tt("dve", g[0:L, 9:10], g[0:L, 9:10], PA[0:L, 256:257], MAX, [g, PA], [g])
                    tt("dve", g[0:L, 10:11], g[0:L, 9:10], g[0:L, 8:9], MAX, [g], [g])
                    recip(g[0:L, 11:12], g[0:L, 10:11], [g], [g])
                    act(junk[0:L, 0:256], PA[0:L, 0:256], AF.Square, [PA], [junk, g], accum=g[0:L, 12:13])
                    stt("dve", g[0:L, 13:14], g[0:L, 12:13], g[0:L, 11:12], g[0:L, 11:12], MUL, MUL, [g], [g])
                    rstd_from_ss(L, 13, 15, 256.0)
                    tt("dve", g[0:L, 16:17], g[0:L, 15:16], g[0:L, 11:12], MUL, [g], [g])
                    stt("dve", ym[0:L, 0:256], PA[0:L, 0:256], g[0:L, 16:17], G[0:L, :], MUL, MUL, [PA, g, G], [ym])
                    if not smp:
                        mm(PB[:, 0:257], ktok[0:L, :], Vp[0:L, 0:257], [ktok, Vp], [PB])
                        stt("dve", Cext[:, :], Cext[:, :], gb[:, 3:4], PB[:, 0:257], MUL, ADD, [Cext, gb, PB], [Cext])
                        act(Cbf[:, :], Cext[:, :], AF.Copy, [Cext], [Cbf])
                        cp("dve", mb[:, :], gb[:, 4:5], [gb], [mb])
                    out_proj(t, L, 2, wm, [wm[:, l * 8 + 2 * h + j:l * 8 + 2 * h + j + 1] for j in range(2)])
                if last_prompt:
                    stv(pC[l, h], Cext[:, 0:256], Cext, pC)
                    stv(pn[l, h:h + 1, :].rearrange("o d -> d o"), Cext[:, 256:257], Cext, pn, allow_slow_non_contiguous=True)
                    stv(pm[l:l + 1, h:h + 1], mb[0:1, 0:1], mb, pm)
            else:
                if first:
                    k.op("pool", lambda e: e.memset(Sst[:], 0.0), [], [Sst])
                    k.op("pool", lambda e: e.memset(Sbf[:], 0.0), [], [Sbf])
                    k.op("pool", lambda e: e.memset(xc[:], 0.0), [], [xc])
                cwq = l * 96
                for t in tiles:
                    L, tok = tinfo(t)
                    smp = (t == NT)
                    if smp:
                        k.dma(CsG[:, :, :], sS[l, :, h].rearrange("s d e -> d s e"), [sS], [Cs] + CsB, Cs)
                        if h == 0:
                            k.dma(oconv[l, :, 0:2, :], sconv[l, :, 1:3, :], [sconv], [oconv], oconv)
                    for p in range(3):
                        for kt in range(8):
                            mm(PA[:, p * 128:p * 128 + L], W[:, kt, p * 128:(p + 1) * 128], hT_t[:, kt, tok:tok + L], [hT[t], W], [PA], start=(kt == 0), stop=(kt == 7))
                    for kt in range(8):
                        mm(PB[0:L, 0:130], hT_t[:, kt, tok:tok + L], W[:, kt, 384:514], [hT[t], W], [PB], start=(kt == 0), stop=(kt == 7))
                    if not smp:
                        for p in range(3):
                            act(xc[:, p, 3:3 + L], PA[:, p * 128:p * 128 + L], AF.Copy, [PA], [xc])
                        taps = [[xc[:, p, j:j + L] for j in range(4)] for p in range(3)]
                        creads = [xc]
                    else:
                        for p in range(3):
                            ld(hist[:, :, :], sconv[l, :, :, p * 1024 + h * 128:p * 1024 + (h + 1) * 128], sconv, hist)
                            for j in range(3):
                                tr(PC[:, (p * 3 + j) * 16:(p * 3 + j + 1) * 16], hist[0:16, j, :], ident[0:16, 0:16], [hist, ident], [PC])
                        cp("dve", histT[:, :, :], PC[:, 0:144].rearrange("p (j s) -> p j s", s=16), [PC], [histT])
                        for p in range(3):
                            act(pfT[:, p, :], PA[:, p * 128:p * 128 + 16], AF.Copy, [PA], [pfT])
                        taps = [[histT[:, p * 3 + j, :] for j in range(3)] + [pfT[:, p, :]] for p in range(3)]
                        creads = [histT, pfT]
                        for p in range(3):
                            stv(oconv[l, :, 2, p * 1024 + h * 128:p * 1024 + (h + 1) * 128].rearrange("s c -> c s"), pfT[:, p, :], pfT, oconv, allow_slow_non_contiguous=True)
                    for p in range(3):
                        wc = lambda j, p=p: cw[:, cwq + (p * 8 + h) * 4 + j:cwq + (p * 8 + h) * 4 + j + 1]
                        ts("dve", cacc[:, p, 0:L], taps[p][0], wc(0), MUL, creads + [cw], [cacc])
                        for j in range(1, 4):
                            stt("dve", cacc[:, p, 0:L], taps[p][j], wc(j), cacc[:, p, 0:L], MUL, ADD, creads + [cw, cacc], [cacc])
                    if not smp:
                        if t == NT - 1:
                            for p in range(3):
                                stv(pconv[l, :, p * 1024 + h * 128:p * 1024 + (h + 1) * 128].rearrange("j c -> c j"), xc[:, p, L:L + 3], xc, pconv, allow_slow_non_contiguous=True)
                        cp("dve", ctmp[:, 0:9].rearrange("c (p j) -> c p j", j=3), xc[:, :, L:L + 3], [xc], [ctmp])
                        cp("dve", xc[:, :, 0:3], ctmp[:, 0:9].rearrange("c (p j) -> c p j", j=3), [ctmp], [xc])
                    act(cs[:, :, 0:L], cacc[:, :, 0:L], AF.Silu, [cacc], [cs])
                    act(G[0:L, 0:128], PB[0:L, 0:128], AF.Silu, [PB], [G])
                    tt("dve", sq[:, 0:2 * L].rearrange("c (p t) -> c p t", p=2), cs[:, 0:2, 0:L], cs[:, 0:2, 0:L], MUL, [cs], [sq])
                    mm(PB[:, 256:256 + 2 * L], onesb[:, :], sq[:, 0:2 * L], [onesb, sq], [PB])
                    act(rs[:, 0:2 * L], PB[:, 256:256 + 2 * L], AF.Ln, [PB], [rs], bias=EPS)
                    act(rs[:, 0:2 * L], rs[:, 0:2 * L], AF.Exp, [rs], [rs], scale=-0.5)
                    stt("dve", qT[:, 0:L], cs[:, 0, 0:L], 128.0 ** -0.5, rs[:, 0:L], MUL, MUL, [cs, rs], [qT])
                    tt("dve", kT[:, 0:L], cs[:, 1, 0:L], rs[:, L:2 * L], MUL, [cs, rs], [kT])
                    act(vTb[:, 0:L], cs[:, 2, 0:L], AF.Copy, [cs], [vTb])
                    tr(PT[0:L, 0, :], kT[:, 0:L], identb[:, :], [kT, identb], [PT])
                    tr(PT[0:L, 1, :], vTb[:, 0:L], identb[:, :], [vTb, identb], [PT])
                    cp("dve", ktok[0:L, :], PT[0:L, 0, :], [PT], [ktok])
                    cp("dve", vtok[0:L, 0:128], PT[0:L, 1, :], [PT], [vtok])
                    act(g[0:L, 0:1], PB[0:L, 128:129], AF.Exp, [PB], [g], scale=-1.0)
                    ts("dve", g[0:L, 0:1], g[0:L, 0:1], 1.0, ADD, [g], [g])
                    recip(g[0:L, 1:2], g[0:L, 0:1], [g], [g])
                    act(g[0:L, 2:3], PB[0:L, 129:130], AF.Exp, [PB, gpd], [g], bias=gpd[0:L, gq + 16 + h:gq + 17 + h])
                    act(g[0:L, 3:4], g[0:L, 2:3], AF.Ln, [g], [g], bias=1.0)
                    ts("dve", g[0:L, 5:6], g[0:L, 3:4], gpd[0:L, gq + 8 + h:gq + 9 + h], MUL, [g, gpd], [g])
                    if not smp:
                        mm(PB[0:L, 130:131], utincl[0:L, 0:L], g[0:L, 5:6], [utincl, g], [PB])
                        mm(PB[:, 131:132], ones[0:L, :], g[0:L, 5:6], [ones, g], [PB])
                        act(g[0:L, 6:7], PB[0:L, 130:131], AF.Exp, [PB], [g])
                        ts("dve", g[0:L, 7:8], g[0:L, 6:7], -1.0, MUL, [g], [g])
                        act(gb[:, 5:6], PB[:, 131:132], AF.Exp, [PB], [gb])
                        cp("dve", gb[:, 6:7], PB[:, 131:132], [PB], [gb])
                        act(g[0:L, 8:9], PB[0:L, 130:131], AF.Exp, [PB, gb], [g], bias=gb[0:L, 6:7], scale=-1.0)
                        ts("dve", gl[0:L, 0:L], ltstr[0:L, 0:L], g[0:L, 5:6], MUL, [ltstr, g], [gl])
                        mm(PC[0:L, 0:L], gl[0:L, 0:L], utincl[0:L, 0:L], [gl, utincl], [PC])
                        act(GamT[0:L, 0:L], PC[0:L, 0:L], AF.Exp, [PC], [GamT])
                        mm(PB[0:L, 256:256 + L], kT[:, 0:L], kT[:, 0:L], [kT], [PB])
                        mm(PB[0:L, 384:384 + L], kT[:, 0:L], qT[:, 0:L], [kT, qT], [PB])
                        tt("dve", t1[0:L, 0:L], PB[0:L, 256:256 + L], GamT[0:L, 0:L], MUL, [PB, GamT], [t1])
                        stt("dve", Um[0][0:L, 0:L], t1[0:L, 0:L], g[0:L, 1:2], utstr[0:L, 0:L], MUL, MUL, [t1, g, utstr], [Um[0]])
                        tt("dve", t1[0:L, 0:L], PB[0:L, 384:384 + L], GamT[0:L, 0:L], MUL, [PB, GamT], [t1])
                        tt("dve", QKG[0:L, 0:L], t1[0:L, 0:L], utincl[0:L, 0:L], MUL, [t1, utincl], [QKG])
                        idn_i = ident if INV_DT == F32 else identb
                        if INV_DT == F32:
                            tr(PC[0:L, 128:128 + L], Um[0][0:L, 0:L], idn_i[0:L, 0:L], [Um[0], idn_i], [PC])
                            cp("dve", Nm[0][0:L, 0:L], PC[0:L, 128:128 + L], [PC], [Nm[0]])
                        else:
                            tr(PT[0:L, 2, 0:L], Um[0][0:L, 0:L], idn_i[0:L, 0:L], [Um[0], idn_i], [PT])
                            cp("dve", Nm[0][0:L, 0:L], PT[0:L, 2, 0:L], [PT], [Nm[0]])
                        stt("dve", Pm[0:L, 0:L], Um[0][0:L, 0:L], -1.0, ident[0:L, 0:L], MUL, ADD, [Um[0], ident], [Pm])
                        nlev = {128: 6, 16: 3}[L]
                        cur = 0
                        for lev in range(nlev):
                            nx = 1 - cur
                            mm(PC[0:L, 128:128 + L], Um[cur][0:L, 0:L], Nm[cur][0:L, 0:L], [Um[cur], Nm[cur]], [PC])
                            act(Nm[nx][0:L, 0:L], PC[0:L, 128:128 + L], AF.Copy, [PC], [Nm[nx]])
                            if lev < nlev - 1:
                                mm(PC[0:L, 256:256 + L], Nm[cur][0:L, 0:L], Um[cur][0:L, 0:L], [Um[cur], Nm[cur]], [PC])
                                act(Um[nx][0:L, 0:L], PC[0:L, 256:256 + L], AF.Copy, [PC], [Um[nx]])
                            mm(PC[0:L, 384:384 + L], Nm[nx][0:L, 0:L], Pm[0:L, 0:L], [Nm[nx], Pm], [PC])
                            tt("dve", Pm[0:L, 0:L], Pm[0:L, 0:L], PC[0:L, 384:384 + L], ADD, [Pm, PC], [Pm])
                            cur = nx
                        mm(PA[0:L, 0:128], kT[:, 0:L], Sbf[:, 0:128], [kT, Sbf], [PA])
                        stt("dve", Rp[0:L, :], PA[0:L, 0:128], g[0:L, 7:8], vtok[0:L, 0:128], MUL, ADD, [PA, g, vtok], [Rp])
                        mm(PA[0:L, 128:256], Pm[0:L, 0:L], Rp[0:L, :], [Pm, Rp], [PA])
                        ts("dve", vnew[0:L, :], PA[0:L, 128:256], g[0:L, 1:2], MUL, [PA, g], [vnew])
                        mm(PA[0:L, 256:384], qT[:, 0:L], Sbf[:, 0:128], [qT, Sbf], [PA])
                        ts("dve", osb[0:L, :], PA[0:L, 256:384], g[0:L, 6:7], MUL, [PA, g], [osb])
                        mm(PA[0:L, 384:512], QKG[0:L, 0:L], vnew[0:L, :], [QKG, vnew], [PA])
                        tt("dve", osb[0:L, :], osb[0:L, :], PA[0:L, 384:512], ADD, [osb, PA], [osb])
                        ts("dve", kd[0:L, :], ktok[0:L, :], g[0:L, 8:9], MUL, [ktok, g], [kd])
                        mm(PC[:, 0:128], kd[0:L, :], vnew[0:L, :], [kd, vnew], [PC])
                        stt("dve", Sst[:, 0:128], Sst[:, 0:128], gb[:, 5:6], PC[:, 0:128], MUL, ADD, [Sst, gb, PC], [Sst])
                        act(Sbf[:, 0:128], Sst[:, 0:128], AF.Copy, [Sst], [Sbf])
                    else:
                        act(g[0:16, 6:7], g[0:16, 5:6], AF.Exp, [g], [g])
                        ts("dve", g[0:16, 7:8], g[0:16, 6:7], -1.0, MUL, [g], [g])
                        ts("dve", dg16[:, :], ident[0:16, 0:16], g[0:16, 6:7], MUL, [ident, g], [dg16])
                        mm(PC[:, 384:400], ones[0:16, :], dg16[:, :], [ones, dg16], [PC])
                        cp("dve", decb[:, :], PC[:, 384:400], [PC], [decb])
                        cp("dve", qTf[:, :], qT[:, 0:16], [qT], [qTf])
                        cp("dve", kTf[:, :], kT[:, 0:16], [kT], [kTf])
                        tt("dve", QTm[:, :, :], eye16[:, :, :], qTf[:, 0:16].rearrange("p (o s) -> p o s", o=1).to_broadcast([128, 16, 16]), MUL, [eye16, qTf], [QTm])
                        tt("dve", KTm[:, :, :], eye16[:, :, :], kTf[:, 0:16].rearrange("p (o s) -> p o s", o=1).to_broadcast([128, 16, 16]), MUL, [eye16, kTf], [KTm])
                        for i in range(16):
                            mm(PA[0:16, 0:128], KTm[:, i, :], CsG[:, i, :], [KTm, CsB[i]], [PA], start=(i == 0), stop=(i == 15))
                        stt("dve", Rp[0:16, :], PA[0:16, 0:128], g[0:16, 7:8], vtok[0:16, 0:128], MUL, ADD, [PA, g, vtok], [Rp])
                        ts("dve", vnew[0:16, :], Rp[0:16, :], g[0:16, 1:2], MUL, [Rp, g], [vnew])
                        for i in range(16):
                            Kmask = Kmask2[i % 2]
                            PX, pxa = (PC, PC[:, 256:384]) if i % 2 == 0 else (PB, PB[:, 256:384])
                            ts("dve", Kmask[:, :], ktok[0:16, :], ident[0:16, i:i + 1], MUL, [ktok, ident], [Kmask])
                            mm(pxa, Kmask[:, :], vnew[0:16, :], [Kmask, vnew], [PX])
                            stt("dve", CsG[:, i, :], CsG[:, i, :], decb[:, i:i + 1], pxa, MUL, ADD, [CsB[i], decb, PX], [CsB[i]])
                            mm(PA[0:16, 256:384], QTm[:, i, :], CsG[:, i, :], [QTm, CsB[i]], [PA], start=(i == 0), stop=(i == 15))
                        k.dma(oS[l, :, h].rearrange("s d e -> d s e"), CsG[:, :, :], [Cs] + CsB, [oS], Cs)
                        cp("dve", osb[0:16, :], PA[0:16, 256:384], [PA], [osb])
                    act(junk[0:L, 0:128], osb[0:L, :], AF.Square, [osb], [junk, g], accum=g[0:L, 12:13])
                    rstd_from_ss(L, 12, 15, 128.0)
                    stt("dve", ym[0:L, 0:128], osb[0:L, :], g[0:L, 15:16], G[0:L, 0:128], MUL, MUL, [osb, g, G], [ym])
                    out_proj(t, L, 1, wg, [wg[:, l:l + 1]])
                if last_prompt:
                    stv(pS[l, h], Sst[:, 0:128], Sst, pS)

        prompt_tiles = list(range(NT))
        pi = 0
        while pi < len(phases):
            pair = phases[pi:pi + 2]
            if pair[0][1] == "m" and pair[0][2] == 0:
                prologue(pair[0][0])
            streams = []
            for li, (l, kind, h) in enumerate(pair):
                load_weights(lanes[li], l, kind, h)
            for li, (l, kind, h) in enumerate(pair):
                k.rec = []
                run_phase(lanes[li], l, kind, h, prompt_tiles)
                streams.append(k.rec)
                k.rec = None
            k.play(streams)
            if do_samples:
                for li, (l, kind, h) in enumerate(pair):
                    run_phase(lanes[li], l, kind, h, [NT])
            pi += 2
        g = lanes[0].g; junk = xn; mb = lanes[0].mb
        k.dma(CsM[:, 0:4, 0:256], fnw[:, :].rearrange("p (a b) -> p a b", a=4), [fnw], [Cs] + CsB, Cs)
        for t in all_tiles:
            L, tok = tinfo(t)
            act(junk[0:L, :], xt[t][0:L, :], AF.Square, [xt[t]], [junk, g], accum=g[0:L, 0:1])
            act(g[0:L, 1:2], g[0:L, 0:1], AF.Ln, [g], [g], bias=EPS, scale=1.0 / 1024.0)
            act(g[0:L, 1:2], g[0:L, 1:2], AF.Exp, [g], [g], scale=-0.5)
            for a4 in range(4):
                stt("dve", xt[t][0:L, a4 * 256:(a4 + 1) * 256], xt[t][0:L, a4 * 256:(a4 + 1) * 256], g[0:L, 1:2], CsM[0:L, a4, 0:256], MUL, MUL, [xt[t], g, Cs], [xt[t]])
            if t == NT:
                stv(ys[:, :], xt[t][0:16, :], xt[t], ys)
            elif t > 0:
                stv(yp[tok - 16:tok - 16 + L, :], xt[t][0:L, :], xt[t], yp)
        allb = xt + [Cs, pfT, oconv] + [b for ln in lanes for b in (ln.Cext, ln.mb, ln.Sst, ln.xc, ln.g)]
        k.final_wait("sp", allb)
        print('total ops', k.nops)
        k.emit()
    return nc


def _consts():
    i = np.arange(128)
    c = {}
    c["cident"] = np.eye(128, dtype=np.float32)
    c["cutincl"] = (i[:, None] <= i[None, :]).astype(np.float32)
    c["cutstr"] = (i[:, None] < i[None, :]).astype(np.float32)
    c["cltstr"] = (i[:, None] > i[None, :]).astype(np.float32)
    c["ceye16"] = np.ascontiguousarray(np.broadcast_to(np.eye(16, dtype=np.float32)[None], (128, 16, 16)))
    return c


def _pk(w):
    C = w.shape[1]
    return w.reshape(8, 128, C).transpose(1, 0, 2)


_NC_CACHE = {}


def kernel(x_prompt, x_sample, state_mlstm_C, state_mlstm_n, state_mlstm_m, state_gdn_S, state_gdn_conv,
           meta_tokens, norm_w, w_in, b_igate, b_fgate, w_mlstm_norm, conv_w, a_log, dt_bias,
           w_gdn_norm, w_out, final_norm_w, _n_layers=DEPTH, _do_samples=True, _max_phases=None):
    f = lambda a: np.ascontiguousarray(np.asarray(a, dtype=np.float32))
    x_prompt, x_sample, meta_tokens, w_in, w_out = f(x_prompt), f(x_sample), f(meta_tokens), f(w_in), f(w_out)
    OFF = dict(qm=0, km=512, vm=1024, om=2048, zm=3072, im=4096, fm=4100, qg=4104, kg=5128, vg=6152, zg=7176, bg=8200, ag=8208)
    WM = np.empty((DEPTH, 4, 128, 8, 1026), np.float32)
    WG = np.empty((DEPTH, 8, 128, 8, 514), np.float32)
    for l in range(DEPTH):
        w = w_in[l]
        for h in range(4):
            cols = np.concatenate([np.arange(OFF["qm"] + h * 128, OFF["qm"] + (h + 1) * 128),
                                   np.arange(OFF["km"] + h * 128, OFF["km"] + (h + 1) * 128),
                                   np.arange(OFF["vm"] + h * 256, OFF["vm"] + (h + 1) * 256),
                                   np.arange(OFF["om"] + h * 256, OFF["om"] + (h + 1) * 256),
                                   np.arange(OFF["zm"] + h * 256, OFF["zm"] + (h + 1) * 256),
                                   [OFF["im"] + h, OFF["fm"] + h]]).astype(np.int64)
            WM[l, h] = _pk(w[:, cols])
        for h in range(8):
            cols = np.concatenate([np.arange(OFF[kk] + h * 128, OFF[kk] + (h + 1) * 128) for kk in ("qg", "kg", "vg", "zg")]
                                  + [[OFF["bg"] + h, OFF["ag"] + h]]).astype(np.int64)
            WG[l, h] = _pk(w[:, cols])
    WO = np.ascontiguousarray(w_out.reshape(DEPTH, 16, 128, 1024))
    nwT = np.ascontiguousarray(f(norm_w).reshape(DEPTH, 8, 128).transpose(2, 0, 1).reshape(128, DEPTH * 8))
    wmT = np.ascontiguousarray(f(w_mlstm_norm).reshape(DEPTH, 8, 128).transpose(2, 0, 1).reshape(128, DEPTH * 8))
    wgT = np.ascontiguousarray(f(w_gdn_norm).T)
    cwT = np.ascontiguousarray(f(conv_w).reshape(DEPTH, 4, 3, 8, 128).transpose(4, 0, 2, 3, 1).reshape(128, DEPTH * 96))
    gpr = np.concatenate([f(b_igate), f(b_fgate), f(a_log), f(dt_bias)], axis=1).reshape(1, DEPTH * 24)
    gpb = np.ascontiguousarray(np.broadcast_to(gpr, (128, DEPTH * 24)))
    fnw = np.ascontiguousarray(np.broadcast_to(f(final_norm_w)[None], (128, 1024)))
    consts = _consts()
    key = (_n_layers, _do_samples, _max_phases)
    if key not in _NC_CACHE:
        _NC_CACHE[key] = build_program(_n_layers, _do_samples, _max_phases)
    nc = _NC_CACHE[key]
    sC_, sn_, sm_, sS_, sv_ = f(state_mlstm_C), f(state_mlstm_n), f(state_mlstm_m), f(state_gdn_S), f(state_gdn_conv)
    in_maps = []
    for c in range(8):
        sl = slice(16 * c, 16 * c + 16)
        m = dict(xp=np.concatenate([meta_tokens, x_prompt[c]], axis=0), xsm=np.ascontiguousarray(x_sample[sl, 0, :]),
                 WM=WM, WG=WG, WO=WO,
                 sC=np.ascontiguousarray(sC_[:, sl]), sn=np.ascontiguousarray(sn_[:, sl]), sm=np.ascontiguousarray(sm_[:, sl]),
                 sS=np.ascontiguousarray(sS_[:, sl]), sconv=np.ascontiguousarray(sv_[:, sl]),
                 nwT=nwT, wmT=wmT, wgT=wgT, cwT=cwT, gpb=gpb, fnw=fnw)
        m.update(consts)
        in_maps.append(m)
    res = run_bass_kernel_spmd(nc, in_maps, core_ids=list(range(8)))
    R = res.results
    cat = lambda name, ax: np.concatenate([np.expand_dims(r[name], ax) if False else r[name] for r in R], axis=ax)
    y_prompt = np.stack([r["yp"] for r in R], 0)
    y_sample = np.concatenate([r["ys"] for r in R], 0)[:, None, :]
    p_C = np.stack([r["pC"] for r in R], 1); p_n = np.stack([r["pn"] for r in R], 1); p_m = np.stack([r["pm"] for r in R], 1)
    p_S = np.stack([r["pS"] for r in R], 1); p_conv = np.stack([r["pconv"] for r in R], 1)
    s_C = np.concatenate([r["oC"] for r in R], 1); s_n = np.concatenate([r["on"] for r in R], 1)
    s_m = np.concatenate([r["om"] for r in R], 1); s_S = np.concatenate([r["oS"] for r in R], 1)
    s_conv = np.concatenate([r["oconv"] for r in R], 1)
    outs = (y_prompt, y_sample, p_C, p_n, p_m, p_S, p_conv, s_C, s_n, s_m, s_S, s_conv)
    return tuple(np.ascontiguousarray(o.astype(np.float32)) for o in outs)
```
